# Optimizing a Trainium2 kernel written in Bass

```python
import math
import jax, jax.numpy as jnp
from jax import lax
import numpy as np

D_MODEL = 1024
BATCH = 32
SEQ = 2048
DEPTH = 4
DEC_BATCH = 1
DEC_SEQ = 16384
PAST_LEN = 128

HEAD_DIM = 64
GROUP_HEADS = 4
GROUP_WIDTH = GROUP_HEADS * HEAD_DIM
N_MIXERS = 4
MIX_WIDTH = N_MIXERS * GROUP_WIDTH
A_KV_HEADS = 2
B_KV_HEADS = 2
GRID_W = 64
ROPE_THETA = 10000.0
Q_BLOCK = 128
B_RADIUS = 128
C_BRANCHES = ((128, 1), (512, 4), (2048, 16))
NA_ROWS = 8
NA_COLS = 16
NA_COL_BLOCK = 16
NA_COL_BAND = 32
MEM_LEN = 256
X_HEADS = 4
X_HEAD_DIM = D_MODEL // X_HEADS
D_FF = 2816
CONV_W = 3
EPS = 1e-6
NEG = -1e30

IN_SIZES = (GROUP_WIDTH, A_KV_HEADS * HEAD_DIM, A_KV_HEADS * HEAD_DIM,
            GROUP_WIDTH, B_KV_HEADS * HEAD_DIM, B_KV_HEADS * HEAD_DIM,
            GROUP_WIDTH, GROUP_WIDTH, GROUP_WIDTH,
            GROUP_WIDTH, GROUP_WIDTH, GROUP_WIDTH)
IN_WIDTH = sum(IN_SIZES)
IN_SPLITS = tuple(int(s) for s in np.cumsum(IN_SIZES)[:-1])

kernel_name = "hybrid_parallel_group_encoder"


def rms_norm(x, g):
    xf = x.astype(jnp.float32)
    y = xf * lax.rsqrt(jnp.mean(xf * xf, axis=-1, keepdims=True) + EPS)
    return (y * g.astype(jnp.float32)).astype(x.dtype)


def rope(x, pos):
    half = x.shape[-1] // 2
    freqs = ROPE_THETA ** (-jnp.arange(half, dtype=jnp.float32) / half)
    ang = pos[:, None] * freqs[None, :]
    cos = jnp.cos(ang)[None, :, None, :]
    sin = jnp.sin(ang)[None, :, None, :]
    xf = x.astype(jnp.float32)
    x1, x2 = xf[..., :half], xf[..., half:]
    return jnp.concatenate([x1 * cos - x2 * sin, x2 * cos + x1 * sin], axis=-1).astype(x.dtype)


def banded_attention(q, k, v, radius, block, sink=None):
    B, L, H, D = q.shape
    KVH = k.shape[2]
    G = H // KVH
    nb = -(-L // block)
    Lp = nb * block
    width = block + 2 * radius
    qp = jnp.pad(q, ((0, 0), (0, Lp - L), (0, 0), (0, 0)))
    pad_kv = ((0, 0), (radius, Lp - L + radius), (0, 0), (0, 0))
    kp = jnp.pad(k, pad_kv)
    vp = jnp.pad(v, pad_kv)
    idx = jnp.arange(nb)[:, None] * block + jnp.arange(width)[None, :]
    kb = kp[:, idx]
    vb = vp[:, idx]
    qb = qp.reshape(B, nb, block, KVH, G, D)
    s = jnp.einsum('bnqkgd,bnwkd->bnkgqw', qb, kb).astype(jnp.float32) * (D ** -0.5)
    qpos = jnp.arange(nb)[:, None] * block + jnp.arange(block)[None, :]
    kpos = idx - radius
    dist = jnp.abs(kpos[:, None, :] - qpos[:, :, None])
    valid = (dist <= radius) & (kpos[:, None, :] >= 0) & (kpos[:, None, :] < L)
    s = jnp.where(valid[None, :, None, None], s, NEG)
    lse = jax.nn.logsumexp(s, axis=-1)
    if sink is not None:
        lse = jnp.logaddexp(lse, sink.astype(jnp.float32).reshape(KVH, G)[None, None, :, :, None])
    p = jnp.exp(s - lse[..., None])
    o = jnp.einsum('bnkgqw,bnwkd->bnqkgd', p.astype(v.dtype), vb).reshape(B, Lp, H, D)[:, :L]
    lse = lse.transpose(0, 1, 4, 2, 3).reshape(B, Lp, H)[:, :L]
    return o, lse


def global_axial_attention(q, k, v, q_g, k_g):
    B, T, H, D = q.shape
    KVH = k.shape[2]
    G = H // KVH
    t = jnp.arange(T)
    row = (t // GRID_W).astype(jnp.float32)
    col = (t % GRID_W).astype(jnp.float32)
    half = D // 2
    q = rms_norm(q, q_g)
    k = rms_norm(k, k_g)
    q = jnp.concatenate([rope(q[..., :half], row), rope(q[..., half:], col)], axis=-1)
    k = jnp.concatenate([rope(k[..., :half], row), rope(k[..., half:], col)], axis=-1)
    nb = T // Q_BLOCK
    qb = q.reshape(B, nb, Q_BLOCK, KVH, G, D).transpose(1, 0, 2, 3, 4, 5)
    scale = D ** -0.5

    def one_block(qi):
        s = jnp.einsum('bqkgd,bskd->bkgqs', qi, k).astype(jnp.float32) * scale
        p = jax.nn.softmax(s, axis=-1)
        return jnp.einsum('bkgqs,bskd->bqkgd', p.astype(v.dtype), v)

    out = lax.map(one_block, qb)
    return out.transpose(1, 0, 2, 3, 4, 5).reshape(B, T, H, D)


def window_sink_attention(q, k, v, sink):
    pos = jnp.arange(q.shape[1], dtype=jnp.float32)
    out, _ = banded_attention(rope(q, pos), rope(k, pos), v, B_RADIUS, Q_BLOCK, sink)
    return out


def _to_sub(x, d):
    B, T, H, D = x.shape
    return x.reshape(B, T // d, d, H, D).transpose(0, 2, 1, 3, 4).reshape(B * d, T // d, H, D)


def dilated_attention(q, k, v):
    B, T, H, D = q.shape
    pos = jnp.arange(T, dtype=jnp.float32)
    q = rope(q, pos)
    k = rope(k, pos)
    outs, lses = [], []
    for window, d in C_BRANCHES:
        radius = window // (2 * d)
        o, l = banded_attention(_to_sub(q, d), _to_sub(k, d), _to_sub(v, d), radius, radius)
        outs.append(o.reshape(B, d, T // d, H, D).transpose(0, 2, 1, 3, 4).reshape(B, T, H, D))
        lses.append(l.reshape(B, d, T // d, H).transpose(0, 2, 1, 3).reshape(B, T, H))
    w = jax.nn.softmax(jnp.stack(lses, axis=0), axis=0)
    return jnp.einsum('nbth,nbthd->bthd', w.astype(q.dtype), jnp.stack(outs, axis=0))


def neighbourhood_attention(q, k, v, rpb):
    B, T, H, D = q.shape
    rows = T // GRID_W
    kh = min(NA_ROWS, rows)
    ncb = GRID_W // NA_COL_BLOCK
    r = jnp.arange(rows)
    row_start = jnp.clip(r - kh // 2, 0, rows - kh)
    row_idx = row_start[:, None] + jnp.arange(kh)[None, :]
    band_start = jnp.clip(jnp.arange(ncb) * NA_COL_BLOCK - NA_COLS // 2, 0, GRID_W - NA_COL_BAND)
    col_idx = band_start[:, None] + jnp.arange(NA_COL_BAND)[None, :]
    kg = k.reshape(B, rows, GRID_W, H, D)
    vg = v.reshape(B, rows, GRID_W, H, D)
    ri = row_idx[:, None, :, None]
    ci = col_idx[None, :, None, :]
    kb = kg[:, ri, ci]
    vb = vg[:, ri, ci]
    qb = q.reshape(B, rows, ncb, NA_COL_BLOCK, H, D)
    s = jnp.einsum('brcjhd,brcmnhd->bhrcjmn', qb, kb).astype(jnp.float32) * (D ** -0.5)
    qcol = jnp.arange(GRID_W).reshape(ncb, NA_COL_BLOCK)
    qcol_start = jnp.clip(qcol - NA_COLS // 2, 0, GRID_W - NA_COLS)
    rel = col_idx[:, None, :] - qcol_start[:, :, None]
    col_valid = (rel >= 0) & (rel < NA_COLS)
    dc = jnp.clip(col_idx[:, None, :] - qcol[:, :, None], -(NA_COLS - 1), NA_COLS - 1) + (NA_COLS - 1)
    dr = row_idx - r[:, None] + (NA_ROWS - 1)
    bias = rpb[:, dr[:, None, None, :, None], dc[None, :, :, None, :]]
    s = s + bias[None].astype(jnp.float32)
    s = jnp.where(col_valid[None, None, None, :, :, None, :], s, NEG)
    p = jax.nn.softmax(s.reshape(s.shape[:-2] + (kh * NA_COL_BAND,)), axis=-1).reshape(s.shape)
    o = jnp.einsum('bhrcjmn,brcmnhd->brcjhd', p.astype(v.dtype), vb)
    return o.reshape(B, T, H, D)


def token_mixer(h, w_in, a_q_norm_g, a_k_norm_g, b_sink, d_rpb, grp_norm_g, w_out):
    B, T, _ = h.shape
    proj = h @ w_in
    aq, ak, av, bq, bk, bv, cq, ck, cv, dq, dk, dv = jnp.split(proj, IN_SPLITS, axis=-1)
    heads = lambda z: z.reshape(B, T, -1, HEAD_DIM)
    ya = global_axial_attention(heads(aq), heads(ak), heads(av), a_q_norm_g, a_k_norm_g)
    yb = window_sink_attention(heads(bq), heads(bk), heads(bv), b_sink)
    yc = dilated_attention(heads(cq), heads(ck), heads(cv))
    yd = neighbourhood_attention(heads(dq), heads(dk), heads(dv), d_rpb)
    y = jnp.stack([z.reshape(B, T, GROUP_WIDTH) for z in (ya, yb, yc, yd)], axis=2)
    y = rms_norm(y, grp_norm_g)
    return y.reshape(B, T, MIX_WIDTH) @ w_out


def memory_cross_attention(h, mem, mem_g, w_xq, w_xkv, w_xo):
    B, T, _ = h.shape
    M = mem.shape[1]
    m = rms_norm(mem, mem_g)
    q = (h @ w_xq).reshape(B, T, X_HEADS, X_HEAD_DIM)
    kv = (m @ w_xkv).reshape(B, M, 2, X_HEADS, X_HEAD_DIM)
    k, v = kv[:, :, 0], kv[:, :, 1]
    s = jnp.einsum('bthd,bmhd->bhtm', q, k).astype(jnp.float32) * (X_HEAD_DIM ** -0.5)
    p = jax.nn.softmax(s, axis=-1)
    o = jnp.einsum('bhtm,bmhd->bthd', p.astype(v.dtype), v).reshape(B, T, D_MODEL)
    return o @ w_xo


def conv_glu_ffn(h, w_up, conv_w, conv_b, w_down):
    u = h @ w_up
    gate, val = jnp.split(u, 2, axis=-1)
    gate = lax.conv_general_dilated(gate, conv_w[:, None, :], window_strides=(1,),
                                    padding=((CONV_W // 2, CONV_W // 2),),
                                    dimension_numbers=('NWC', 'WIO', 'NWC'),
                                    feature_group_count=D_FF) + conv_b
    return (jax.nn.gelu(gate, approximate=False) * val) @ w_down


def encoder_trunk(x, mem, norm_mix_g, w_in, a_q_norm_g, a_k_norm_g, b_sink, d_rpb, grp_norm_g, w_out,
                  norm_x_g, norm_mem_g, w_xq, w_xkv, w_xo, norm_ffn_g, w_up, conv_w, conv_b, w_down,
                  final_norm_g):
    for l in range(DEPTH):
        x = x + token_mixer(rms_norm(x, norm_mix_g[l]), w_in[l], a_q_norm_g[l], a_k_norm_g[l],
                            b_sink[l], d_rpb[l], grp_norm_g[l], w_out[l])
        x = x + memory_cross_attention(rms_norm(x, norm_x_g[l]), mem, norm_mem_g[l],
                                       w_xq[l], w_xkv[l], w_xo[l])
        x = x + conv_glu_ffn(rms_norm(x, norm_ffn_g[l]), w_up[l], conv_w[l], conv_b[l], w_down[l])
    return rms_norm(x, final_norm_g)


def setup_inputs(seed: int = 0) -> dict:
    key = jax.random.key(seed)
    ks = jax.random.split(key, 24)
    f32 = jnp.float32

    def w(k, shape, fan_in):
        return jax.random.normal(k, shape, f32) * (fan_in ** -0.5)

    def gain(k, shape):
        return 1.0 + 0.02 * jax.random.normal(k, shape, f32)

    return {
        'x_prompt': jax.random.normal(ks[0], (BATCH, SEQ, D_MODEL), f32),
        'x_sample': jax.random.normal(ks[1], (DEC_BATCH, DEC_SEQ, D_MODEL), f32),
        'mem_prompt': jax.random.normal(ks[2], (BATCH, MEM_LEN, D_MODEL), f32),
        'mem_sample': jax.random.normal(ks[3], (DEC_BATCH, MEM_LEN, D_MODEL), f32),
        'norm_mix_g': gain(ks[4], (DEPTH, D_MODEL)),
        'w_in': w(ks[5], (DEPTH, D_MODEL, IN_WIDTH), D_MODEL),
        'a_q_norm_g': gain(ks[6], (DEPTH, HEAD_DIM)),
        'a_k_norm_g': gain(ks[7], (DEPTH, HEAD_DIM)),
        'b_sink': 0.5 * jax.random.normal(ks[8], (DEPTH, GROUP_HEADS), f32),
        'd_rpb': 0.1 * jax.random.normal(ks[9], (DEPTH, GROUP_HEADS, 2 * NA_ROWS - 1, 2 * NA_COLS - 1), f32),
        'grp_norm_g': gain(ks[10], (DEPTH, N_MIXERS, GROUP_WIDTH)),
        'w_out': w(ks[11], (DEPTH, MIX_WIDTH, D_MODEL), MIX_WIDTH),
        'norm_x_g': gain(ks[12], (DEPTH, D_MODEL)),
        'norm_mem_g': gain(ks[13], (DEPTH, D_MODEL)),
        'w_xq': w(ks[14], (DEPTH, D_MODEL, D_MODEL), D_MODEL),
        'w_xkv': w(ks[15], (DEPTH, D_MODEL, 2 * D_MODEL), D_MODEL),
        'w_xo': w(ks[16], (DEPTH, D_MODEL, D_MODEL), D_MODEL),
        'norm_ffn_g': gain(ks[17], (DEPTH, D_MODEL)),
        'w_up': w(ks[18], (DEPTH, D_MODEL, 2 * D_FF), D_MODEL),
        'conv_w': w(ks[19], (DEPTH, CONV_W, D_FF), CONV_W),
        'conv_b': 0.02 * jax.random.normal(ks[20], (DEPTH, D_FF), f32),
        'w_down': w(ks[21], (DEPTH, D_FF, D_MODEL), D_FF),
        'final_norm_g': gain(ks[22], (D_MODEL,)),
    }


def reference(x_prompt, x_sample, mem_prompt, mem_sample, norm_mix_g, w_in, a_q_norm_g, a_k_norm_g,
              b_sink, d_rpb, grp_norm_g, w_out, norm_x_g, norm_mem_g, w_xq, w_xkv, w_xo, norm_ffn_g,
              w_up, conv_w, conv_b, w_down, final_norm_g):
    y_prompt = encoder_trunk(x_prompt, mem_prompt, norm_mix_g, w_in, a_q_norm_g, a_k_norm_g, b_sink, d_rpb,
                             grp_norm_g, w_out, norm_x_g, norm_mem_g, w_xq, w_xkv, w_xo, norm_ffn_g,
                             w_up, conv_w, conv_b, w_down, final_norm_g)
    y_sample = encoder_trunk(x_sample, mem_sample, norm_mix_g, w_in, a_q_norm_g, a_k_norm_g, b_sink, d_rpb,
                             grp_norm_g, w_out, norm_x_g, norm_mem_g, w_xq, w_xkv, w_xo, norm_ffn_g,
                             w_up, conv_w, conv_b, w_down, final_norm_g)
    return (y_prompt, y_sample)
```

```python
import os
from contextlib import ExitStack
import numpy as np
import concourse.bass as bass
import concourse.mybir as mybir
from concourse.bass_utils import run_bass_kernel_spmd

F32 = mybir.dt.float32
BF16 = mybir.dt.bfloat16
I32 = mybir.dt.int32
AF = mybir.ActivationFunctionType
ALU = mybir.AluOpType
AX = mybir.AxisListType

NCORES = 8
T = 2048
NT = 16
D = 1024
KC = 8
DFF = 2816
NFC = 22
EPS = 1e-6
NSEQ = int(os.environ.get("MK_NSEQ", "5"))
NLAYER = int(os.environ.get("MK_NLAYER", "4"))
SEQ0 = int(os.environ.get("MK_SEQ0", "0"))


class Tok:
    __slots__ = ("sem", "val", "eng")

    def __init__(self, sem, val, eng):
        self.sem, self.val, self.eng = sem, val, eng


class Buf:
    __slots__ = ("name", "w", "r", "excl")

    def __init__(self, name, excl=False):
        self.name, self.w, self.r, self.excl = name, None, {}, excl


class DSem:
    def __init__(self, sem):
        self.sem, self.count = sem, 0


class Eng:
    def __init__(self, name, sem):
        self.name, self.sem, self.count, self.waited, self.prog = name, sem, 0, {}, []


class SBoard:
    def __init__(self):
        self.engs = {}
        self.dsems = []
        self.dry = False

    def add_engine(self, name, sem):
        self.engs[name] = Eng(name, sem)

    def op(self, eng, fn, reads=(), writes=(), dsem=None, inc=None):
        if self.dry:
            return None
        e = self.engs[eng]
        deps = []
        ex = [b for b in reads if b.excl]
        if ex:
            reads = [b for b in reads if not b.excl]
            writes = list(writes) + [b for b in ex if b not in writes]
        for b in reads:
            if b.w is not None:
                deps.append(b.w)
        for b in writes:
            if b.w is not None:
                deps.append(b.w)
            deps.extend(b.r.values())
        for tk in deps:
            if tk.eng == "pe" and eng == "pe":
                continue
            key = id(tk.sem)
            if e.waited.get(key, 0) >= tk.val:
                continue
            e.waited[key] = tk.val
            e.prog.append(("w", tk.sem, tk.val))
        if dsem is None:
            e.count += 1
            tok = Tok(e.sem, e.count, eng)
            e.prog.append(("i", fn, e.sem, 1))
        else:
            step = 16 if inc is None else inc
            dsem.count += (1 if step < 0 else step)
            tok = Tok(dsem.sem, dsem.count, "dma")
            e.prog.append(("i", fn, dsem.sem, step))
        for b in reads:
            b.r[id(tok.sem)] = tok
        for b in writes:
            b.w = tok
            b.r = {}
        return tok

    def barrier(self):
        if self.dry:
            return
        toks = [Tok(e.sem, e.count, e.name) for e in self.engs.values() if e.count > 0]
        toks += [Tok(d.sem, d.count, "dma") for d in self.dsems if d.count > 0]
        for e in self.engs.values():
            for tk in toks:
                if tk.sem is e.sem:
                    continue
                key = id(tk.sem)
                if e.waited.get(key, 0) >= tk.val:
                    continue
                e.waited[key] = tk.val
                e.prog.append(("w", tk.sem, tk.val))

    def final_wait(self, eng):
        e = self.engs[eng]
        for d in self.dsems:
            if d.count > 0 and e.waited.get(id(d.sem), 0) < d.count:
                e.prog.append(("w", d.sem, d.count))
        for o in self.engs.values():
            if o is not e and o.count > 0:
                e.prog.append(("w", o.sem, o.count))


def replay(eng_handle, prog):
    for it in prog:
        if it[0] == "w":
            eng_handle.wait_ge(it[1], it[2])
        else:
            ins = it[1](eng_handle)
            if it[3] < 0:
                ins.then_inc(it[2])
            else:
                ins.then_inc(it[2], it[3])


def _rope_tables(pos, half):
    freqs = (np.float32(10000.0) ** (-np.arange(half, dtype=np.float32) / np.float32(half))).astype(np.float32)
    ang = (pos.astype(np.float32)[:, None] * freqs[None, :]).astype(np.float32)
    return np.cos(ang).astype(np.float32), np.sin(ang).astype(np.float32)


def _tile_layout(a):
    n = a.shape[1]
    return np.ascontiguousarray(a.reshape(NT, 128, n).transpose(1, 0, 2).reshape(128, NT * n))


def _tables(tok0):
    pos = np.arange(T, dtype=np.int64) + tok0
    c, s = _rope_tables(pos, 32)
    tbc = np.concatenate([c, s], axis=1)
    row = pos // 64
    col = pos % 64
    cr, sr = _rope_tables(row, 16)
    cc, sc = _rope_tables(col, 16)
    ta = np.concatenate([cr, cc, sr, sc], axis=1)
    return _tile_layout(tbc), _tile_layout(ta)


def _mask_b():
    k = np.arange(128)[:, None]
    q = np.arange(128)[None, :]
    tiles = []
    for dlt in (-1, 0, 1):
        d = 128 * dlt + k - q
        tiles.append((np.abs(d) <= 128).astype(np.float32))
    return np.concatenate(tiles, axis=1)


def _mask_c():
    k = np.arange(128)[:, None]
    q = np.arange(128)[None, :]
    tiles = []
    for dlt in range(-8, 9):
        d = 128 * dlt + k - q
        m = (np.abs(d) <= 64).astype(np.float32)
        m += ((d % 4 == 0) & (np.abs(d) <= 256)).astype(np.float32)
        m += ((d % 16 == 0) & (np.abs(d) <= 1024)).astype(np.float32)
        tiles.append(m)
    return np.concatenate(tiles, axis=1)


D_CASES = [("int", list(range(-2, 3))), ("q0", list(range(-2, 4))), ("q1", list(range(-2, 3))),
           ("q14", list(range(-2, 3))), ("q15", list(range(-3, 3)))]
D_OFF = {}
_o = 0
for _n, _dl in D_CASES:
    D_OFF[_n] = _o
    _o += len(_dl)
D_NT = _o


def _d_valid(qtok, ktok, rows_total):
    qr, qc = qtok // 64, qtok % 64
    kr, kc = ktok // 64, ktok % 64
    rs = np.clip(qr - 4, 0, rows_total - 8)
    cs = np.clip(qc - 8, 0, 48)
    return ((kr >= rs) & (kr < rs + 8) & (kc >= cs) & (kc < cs + 16) & (ktok >= 0) & (ktok < rows_total * 64))


def _mask_d(tok0, total_tokens):
    rows_total = total_tokens // 64
    tiles = []
    k = np.arange(128)[:, None]
    q = np.arange(128)[None, :]
    for name, dl in D_CASES:
        qt = {"int": 4, "q0": 0, "q1": 1, "q14": 14, "q15": 15}[name]
        for dlt in dl:
            qtok = tok0 + qt * 128 + q
            ktok = tok0 + (qt + dlt) * 128 + k
            tiles.append(_d_valid(qtok, ktok, rows_total).astype(np.float32))
    return np.concatenate(tiles, axis=1)


def _rpb_toeplitz(d_rpb):
    L = d_rpb.shape[0]
    k = np.arange(128)[:, None]
    q = np.arange(128)[None, :]
    out = np.zeros((L, 128, 4, 7, 128), np.float32)
    for di, dlt in enumerate(range(-3, 4)):
        dr = (2 * dlt + k // 64 - q // 64) + 7
        dc = np.clip(k % 64 - q % 64, -15, 15) + 15
        ok = (dr >= 0) & (dr <= 14)
        drc = np.clip(dr, 0, 14)
        g = d_rpb[:, :, drc, dc]
        g = np.where(ok[None, None], g, np.float32(0.0))
        out[:, :, :, di, :] = g.transpose(0, 2, 1, 3)
    return np.ascontiguousarray(out.reshape(L, 128, 4 * 7 * 128))


def build_program():
    nc = bass.Bass("TRN2", target_bir_lowering=False)
    es = ExitStack()

    def din(name, shape, dt=F32):
        return nc.dram_tensor(name, list(shape), dt, kind="ExternalInput").ap()

    xin = din("xin", [5, T, D])
    mem = din("mem", [5, 256, D])
    norm_mix_g = din("norm_mix_g", [4, D])
    w_in = din("w_in", [4, D, 2560])
    aqg = din("aqg", [4, 64])
    akg = din("akg", [4, 64])
    bsink = din("bsink", [4, 4])
    grpg = din("grpg", [4, D])
    w_out = din("w_out", [4, D, D])
    norm_x_g = din("norm_x_g", [4, D])
    norm_mem_g = din("norm_mem_g", [4, D])
    w_xq = din("w_xq", [4, D, D])
    w_xkv = din("w_xkv", [4, D, 2 * D])
    w_xo = din("w_xo", [4, D, D])
    norm_ffn_g = din("norm_ffn_g", [4, D])
    w_up = din("w_up", [4, D, 2 * DFF])
    conv_w = din("conv_w", [4, 3, DFF])
    conv_b = din("conv_b", [4, DFF])
    w_down = din("w_down", [4, DFF, D])
    final_g = din("final_g", [1, D])
    rpbT = din("rpbT", [4, 128, 28 * 128])
    ident_d = din("ident", [128, 128])
    maskB_d = din("maskB", [128, 3 * 128])
    maskC_d = din("maskC", [128, 17 * 128])
    maskD_d = din("maskD", [2, 128, D_NT * 128])
    tabBC_d = din("tabBC", [2, 128, NT * 64])
    tabA_d = din("tabA", [2, 128, NT * 64])
    cvalid_d = din("cvalid", [128, 2])
    cinfo_d = din("cinfo", [1, 2], I32)
    yout = nc.dram_tensor("yout", [5, T, D], F32, kind="ExternalOutput").ap()
    expb_d = nc.dram_tensor("expb_scratch", [4, 128, 28 * 128], BF16).ap()

    GW = {"A": (1, 130), "B": (1, 130), "C": (2, 260), "D": (2, 260)}
    pack_d, gath_d = {}, {}
    for l in range(4):
        for g, (npr, vw) in GW.items():
            cols = npr * T + NT * vw
            pack_d[(l, g)] = nc.dram_tensor(f"pack_{l}{g}", [128, cols], BF16).ap()
            gath_d[(l, g)] = nc.dram_tensor(f"gath_{l}{g}", [NCORES * 128, cols], BF16).ap()
        pack_d[(l, "F")] = nc.dram_tensor(f"pack_{l}F", [128, 16], BF16).ap()
        gath_d[(l, "F")] = nc.dram_tensor(f"gath_{l}F", [NCORES * 128, 16], BF16).ap()

    def sb(name, shape, dt):
        return es.enter_context(nc.sbuf_tensor(name, list(shape), dt))

    def ps(name, shape, dt):
        return es.enter_context(nc.psum_tensor(name, list(shape), dt))

    def sem(name):
        return es.enter_context(nc.semaphore(name))

    S = SBoard()
    for en in ("pe", "act", "dve", "pool", "sp"):
        S.add_engine(en, sem("s_" + en))

    def dsem(name):
        d = DSem(sem("d_" + name))
        S.dsems.append(d)
        return d

    X = sb("X", [128, NT, D], F32)
    XNT = sb("XNT", [128, KC, T + 2], BF16)
    GBC = sb("GBC", [128, D], F32)
    GGRP = sb("GGRP", [128, 256], F32)
    XNB = sb("XNB", [128, 2, D], BF16)
    WS = sb("WS", [128, 2, 2048], BF16)
    IDB = sb("IDB", [128, 128], BF16)
    ONES = sb("ONES", [128, 128], BF16)
    SQJ = sb("SQJ", [128, D], BF16)
    SS = sb("SS", [128, 16], F32)
    LNV = sb("LNV", [128, 16], F32)
    RSTD = sb("RSTD", [128, 16], F32)
    SM = sb("SM", [128, 64], F32)
    CVAL = sb("CVAL", [128, 2], F32)
    CINF = sb("CINF", [1, 2], I32)
    QKG = sb("QKG", [128, 2, 64], F32)
    SINK = sb("SINK", [128, 4], F32)
    ESINK = sb("ESINK", [128, 4], F32)
    CW = sb("CW", [128, 4, NFC, 4], F32)
    AB_N = 35584
    AF_N = 3904
    AB = sb("AB", [128, AB_N], BF16)
    AFa = sb("AFa", [128, AF_N], F32)

    class Carve:
        def __init__(self, t, n):
            self.t, self.n, self.off = t, n, 0

        def reset(self):
            self.off = 0

        def take(self, n):
            a = self.t[:, self.off:self.off + n]
            self.off += n
            assert self.off <= self.n, (self.off, self.n)
            return a

    cb, cf = Carve(AB, AB_N), Carve(AFa, AF_N)
    QT = cb.take(2 * T).rearrange("p (a t) -> p a t", a=2)
    KT = cb.take(2 * 4096).rearrange("p (a t) -> p a t", a=2)
    VEf = cb.take(32 * 260)
    VE = VEf.rearrange("p (j w) -> p j w", j=32)
    WO = cb.take(2048).rearrange("p (k n) -> p k n", k=2)
    QKB = cb.take(512).rearrange("p (a n) -> p a n", a=2)
    PT = cb.take(4 * 512).rearrange("p (a n) -> p a n", a=4)
    YB = cb.take(256)
    YT = cb.take(512).rearrange("p (a k t) -> p a k t", a=2, k=2)
    MB = cb.take(3 * 128)
    MC = cb.take(17 * 128)
    MD01 = cb.take(D_NT * 128)
    EXPB = cb.take(28 * 128).rearrange("p (h d q) -> p h d q", h=4, d=7)
    TM = cf.take(512).rearrange("p (a n) -> p a n", a=2)
    RTA = cf.take(512).rearrange("p (a n) -> p a n", a=2)
    RTB = cf.take(512).rearrange("p (a n) -> p a n", a=2)
    YF = cf.take(256)
    TABBC = cf.take(1024).rearrange("p (t n) -> p t n", t=NT)
    TABA = cf.take(1024).rearrange("p (t n) -> p t n", t=NT)
    RPBS = None
    mix_b_end, mix_f_end = cb.off, cf.off
    cb.reset(); cf.reset()
    QX = cb.take(KC * T).rearrange("p (k t) -> p k t", k=KC)
    MK = cb.take(KC * 256).rearrange("p (k m) -> p k m", k=KC)
    MV = cb.take(2 * D).rearrange("p (j n) -> p j n", j=2)
    MNT = cb.take(KC * 256).rearrange("p (k m) -> p k m", k=KC)
    PTX = cb.take(4 * 512).rearrange("p (a n) -> p a n", a=4)
    MEMX = cf.take(2 * D).rearrange("p (j n) -> p j n", j=2)
    DENR = cf.take(1024).rearrange("p (a n) -> p a n", a=2)
    cb.reset(); cf.reset()
    HT = cb.take(NFC * 1024).rearrange("p (f t) -> p f t", f=NFC)
    WD = cb.take(2 * NFC * 256).rearrange("p (a f n) -> p a f n", a=2, f=NFC)
    UB = cb.take(1024)
    GST = cf.take(2 * 1026).rearrange("p (a n) -> p a n", a=2)
    CT = cf.take(1024)
    cf.reset()
    OUTS = cf.take(2 * D).rearrange("p (a n) -> p a n", a=2)
    FB = [ps(f"F{i}", [128, 512], F32) for i in range(7)]
    T0 = ps("T0", [128, 1024], BF16)

    def B(n):
        return Buf(n)

    Xb = [B(f"X{t}") for t in range(NT)]
    XNTb = [B(f"XNT{t}") for t in range(NT)]
    XNTh = B("XNTh")
    GBCb, GGRPb = B("GBC"), B("GGRP")
    XNBb = [B("XNB0"), B("XNB1")]
    WSb = [B("WS0"), B("WS1")]
    CONSTb = B("const")
    STATb = B("stat")
    QKGb, SINKb, CWb = B("QKG"), B("SINK"), B("CW")
    Fb = [Buf(f"F{i}", excl=True) for i in range(7)]
    _t0 = Buf("T0", excl=True)
    T0b = [_t0, _t0]
    QTb = [B(f"QT{t}") for t in range(NT)]
    KTb = [B(f"KT{j}") for j in range(32)]
    VEb = [B(f"VE{j}") for j in range(32)]
    WOb = B("WO")
    QKBb = [B("QKB0"), B("QKB1")]
    PTb = [B(f"PT{i}") for i in range(4)]
    YBb, YFb = B("YB"), B("YF")
    YTb = [B("YT0"), B("YT1")]
    MDb, EXPBb, TABb = B("MD01"), B("EXPB"), B("TAB")
    TMb = [B("TM0"), B("TM1")]
    RTb = [B("RT0"), B("RT1")]
    QXb = [B(f"QX{t}") for t in range(4)]
    MKb, MVb, MNTb, MEMXb = B("MK"), B("MV"), B("MNT"), B("MEMX")
    DENRb = [B("DENR0"), B("DENR1")]
    HTb = [B(f"HT{f}") for f in range(NFC)]
    WDb = [B("WD0"), B("WD1")]
    UBb, CTb = B("UB"), B("CT")
    GSTb = [B("GST0"), B("GST1")]
    OUTSb = [B("OUTS0"), B("OUTS1")]
    PACKb = {k: B(f"pack{k}") for k in pack_d}
    GATHb = {k: B(f"gath{k}") for k in gath_d}

    d_x = [dsem(f"x{t}") for t in range(NT)]
    d_ws = [dsem("ws0"), dsem("ws1")]
    d_wd = [dsem("wd0"), dsem("wd1")]
    d_misc = dsem("misc")
    d_gbc, d_ggrp, d_wo = dsem("gbc"), dsem("ggrp"), dsem("wo")
    d_out = [dsem("out0"), dsem("out1")]
    d_kv = [dsem("kv0"), dsem("kv1")]
    d_pack = dsem("pack")
    d_cc = dsem("cc")
    d_mem = dsem("mem")
    d_tab = dsem("tab")
    d_expb, d_qkg, d_sink = dsem("expb"), dsem("qkg"), dsem("sink")

    OP = S.op
    regs = {}
    STOP = os.environ.get("MK_STOP", "")

    def stop(name):
        if STOP == name:
            S.dry = True

    def mm(out, lhsT, rhs, start, stop, reads, writes):
        OP("pe", lambda e: e.matmul(out, lhsT=lhsT, rhs=rhs, start=start, stop=stop, skip_group_check=True),
           reads, writes)

    def tr(out, in_, reads, writes):
        OP("pe", lambda e: e.transpose(out, in_, IDB[:]), reads + [CONSTb], writes)

    def act(out, in_, func, reads, writes, **kw):
        OP("act", lambda e: e.activation(out=out, in_=in_, func=func, **kw), reads, writes)

    def dma(eng, out, in_, reads, writes, ds, **kw):
        OP(eng, lambda e: e.dma_start(out=out, in_=in_, **kw), reads, writes, dsem=ds)

    def tt(eng, out, in0, in1, op, reads, writes):
        OP(eng, lambda e: e.tensor_tensor(out=out, in0=in0, in1=in1, op=op), reads, writes)

    def ts(eng, out, in0, s1, s2, op0, op1, reads, writes, **kw):
        if op1 is None:
            OP(eng, lambda e: e.tensor_scalar(out=out, in0=in0, scalar1=s1, scalar2=None, op0=op0, **kw), reads, writes)
        else:
            OP(eng, lambda e: e.tensor_scalar(out=out, in0=in0, scalar1=s1, scalar2=s2, op0=op0, op1=op1, **kw),
               reads, writes)

    def stt(out, in0, scalar, in1, op0, op1, reads, writes, **kw):
        OP("dve", lambda e: e.scalar_tensor_tensor(out=out, in0=in0, scalar=scalar, in1=in1, op0=op0, op1=op1, **kw),
           reads, writes)

    wspecs = []
    wstate = {"i": 0, "issued": 0}

    def w_issue(i):
        if i >= len(wspecs):
            return
        slot = i % 2
        for (dst_fn, src) in wspecs[i]:
            dma("pool", dst_fn(slot), src, [], [WSb[slot]], d_ws[slot])

    def wreq(parts):
        i = wstate["i"]
        wstate["i"] += 1
        if S.dry:
            wspecs.append(parts)
            return i % 2
        while wstate["issued"] <= min(i + 1, len(wspecs) - 1):
            w_issue(wstate["issued"])
            wstate["issued"] += 1
        return i % 2

    def slab256(wap, l, c0, permq=False):
        if permq:
            parts = []
            for pr in range(2):
                for lh in range(2):
                    src = wap[l, :, c0:c0 + 256].rearrange("(k p) (lh pr d) -> p k pr lh d", p=128, lh=2, pr=2)[:, :, pr, lh, :]
                    parts.append((lambda slot, pr=pr, lh=lh: WS[:, slot, :].rearrange("p (k pr lh d) -> p k pr lh d", k=KC, pr=2, lh=2)[:, :, pr, lh, :], src))
            return parts
        src = wap[l, :, c0:c0 + 256].rearrange("(k p) n -> p k n", p=128)
        return [(lambda slot: WS[:, slot, :].rearrange("p (k n) -> p k n", k=KC), src)]

    def load_consts():
        dma("pool", IDB[:], ident_d[:, :], [], [CONSTb], d_misc)
        dma("sp", CVAL[:], cvalid_d[:, :], [], [CONSTb], d_misc)
        for l in range(4):
            for j in range(3):
                dma("sp", CW[:, l, :, j], conv_w[l, j, :].rearrange("(f p) -> p f", p=128), [], [CONSTb], d_misc,
                    allow_slow_non_contiguous=True)
            dma("sp", CW[:, l, :, 3], conv_b[l, :].rearrange("(f p) -> p f", p=128), [], [CONSTb], d_misc,
                allow_slow_non_contiguous=True)
        tmpb = Buf("rpbstage")
        for l in range(4):
            dma("sp", AFa[:, 0:3584], rpbT[l, :, :], [], [tmpb], d_tab)
            act(EXPB.rearrange("p h d q -> p (h d q)"), AFa[:, 0:3584], AF.Exp, [tmpb], [EXPBb])
            dma("sp", expb_d[l, :, :], EXPB.rearrange("p h d q -> p (h d q)"), [EXPBb], [], d_pack)
        OP("pool", lambda e: e.memset(ONES[:], 1.0), [], [CONSTb])
        OP("pool", lambda e: e.memset(XNT[:, :, 0:1], 0.0), [], [XNTh])
        OP("pool", lambda e: e.memset(XNT[:, :, T + 1:T + 2], 0.0), [], [XNTh])

    def load_seq_tables(smp):
        i = 1 if smp else 0
        dma("pool", MB, maskB_d[:, :], [], [MDb], d_tab)
        dma("pool", MC, maskC_d[:, :], [], [MDb], d_tab)
        dma("pool", MD01, maskD_d[i, :, :], [], [MDb], d_tab)
        dma("sp", TABBC.rearrange("p t n -> p (t n)"), tabBC_d[i, :, :], [], [TABb], d_tab)
        dma("sp", TABA.rearrange("p t n -> p (t n)"), tabA_d[i, :, :], [], [TABb], d_tab)

    def load_x(s):
        for t in range(NT):
            dma("sp", X[:, t, :], xin[s, t * 128:(t + 1) * 128, :], [], [Xb[t]], d_x[t])

    def norm_stats(src_fn, n, src_bufs):
        for t in range(n):
            act(SQJ[:], src_fn(t), AF.Square, [src_bufs[t]], [STATb], accum_out=SS[:, t:t + 1])
        act(LNV[:, 0:n], SS[:, 0:n], AF.Ln, [STATb], [STATb], scale=1.0 / D, bias=EPS)
        act(RSTD[:, 0:n], LNV[:, 0:n], AF.Exp, [STATb], [STATb], scale=-0.5)

    def norm_to_T(src_fn, n, src_bufs, g_ap, dst_fn, dst_bufs):
        dma("sp", GBC[:], g_ap.partition_broadcast(128), [], [GBCb], d_gbc)
        norm_stats(src_fn, n, src_bufs)
        for t in range(n):
            sl = t % 2
            stt(XNB[:, sl, :], src_fn(t), RSTD[:, t:t + 1], GBC[:], ALU.mult, ALU.mult,
                [src_bufs[t], STATb, GBCb], [XNBb[sl]])
            for hh in range(2):
                for k4 in range(4):
                    kc = hh * 4 + k4
                    tr(T0[:, kc * 128:(kc + 1) * 128], XNB[:, sl, kc * 128:(kc + 1) * 128], [XNBb[sl]], [T0b[hh]])
                src = T0[:, hh * 512:(hh + 1) * 512].rearrange("p (k t) -> p k t", k=4)
                dst = dst_fn(t)[:, hh * 4:(hh + 1) * 4, :]
                if hh == 0:
                    act(dst, src, AF.Copy, [T0b[hh]], [dst_bufs[t]])
                else:
                    OP("dve", lambda e, dst=dst, src=src: e.tensor_copy(out=dst, in_=src), [T0b[hh]], [dst_bufs[t]])

    fbrr = {"i": 0}

    def next_fb(lo=0, hi=4):
        i = lo + fbrr["i"] % (hi - lo)
        fbrr["i"] += 1
        return i

    def rope_bc(sl, t, H, perm):
        x3 = TM[:, sl, 0:H * 64].rearrange("p (h d) -> p h d", h=H)
        a3 = RTA[:, sl, 0:H * 64].rearrange("p (h d) -> p h d", h=H)
        b3 = RTB[:, sl, 0:H * 64].rearrange("p (h d) -> p h d", h=H)
        cos = TABBC[:, t, 0:32].unsqueeze(1).to_broadcast([128, H, 32])
        sin = TABBC[:, t, 32:64].unsqueeze(1).to_broadcast([128, H, 32])
        R, W = [TMb[sl], TABb], [RTb[sl]]
        tt("pool", a3[:, :, 0:32], x3[:, :, 0:32], cos, ALU.mult, R, W)
        tt("pool", a3[:, :, 32:64], x3[:, :, 32:64], cos, ALU.mult, R, W)
        tt("pool", b3[:, :, 0:32], x3[:, :, 32:64], sin, ALU.mult, R, W)
        tt("pool", b3[:, :, 32:64], x3[:, :, 0:32], sin, ALU.mult, R, W)
        o3 = QKB[:, sl, 0:H * 64].rearrange("p (h d) -> p h d", h=H)
        tt("dve", o3[:, :, 0:32], a3[:, :, 0:32], b3[:, :, 0:32], ALU.subtract, [RTb[sl]], [QKBb[sl]])
        tt("dve", o3[:, :, 32:64], a3[:, :, 32:64], b3[:, :, 32:64], ALU.add, [RTb[sl]], [QKBb[sl]])

    def rope_a(sl, t, H, perm, gi):
        x3 = TM[:, sl, 0:H * 64].rearrange("p (h d) -> p h d", h=H)
        a3 = RTA[:, sl, 0:H * 64].rearrange("p (h d) -> p h d", h=H)
        b3 = RTB[:, sl, 0:H * 64].rearrange("p (h d) -> p h d", h=H)
        R, W = [TMb[sl], TABb], [RTb[sl]]
        tt("pool", a3, x3, x3, ALU.mult, R, W)
        OP("dve", lambda e: e.tensor_reduce(out=SM[:, 0:H], in_=a3, axis=AX.X, op=ALU.add), [RTb[sl]], [STATb])
        act(SM[:, 8:8 + H], SM[:, 0:H], AF.Ln, [STATb], [STATb], scale=1.0 / 64, bias=EPS)
        act(SM[:, 16:16 + H], SM[:, 8:8 + H], AF.Exp, [STATb], [STATb], scale=-0.5)
        tt("dve", b3, x3, SM[:, 16:16 + H].unsqueeze(2).to_broadcast([128, H, 64]), ALU.mult, [TMb[sl], STATb], [RTb[sl]])
        tt("pool", x3, b3, QKG[:, gi, :].unsqueeze(1).to_broadcast([128, H, 64]), ALU.mult, [RTb[sl], QKGb], [TMb[sl]])
        x5 = TM[:, sl, 0:H * 64].rearrange("p (h r f d) -> p h r f d", h=H, r=2, f=2)
        a5 = RTA[:, sl, 0:H * 64].rearrange("p (h r f d) -> p h r f d", h=H, r=2, f=2)
        b5 = RTB[:, sl, 0:H * 64].rearrange("p (h r f d) -> p h r f d", h=H, r=2, f=2)
        cos = TABA[:, t, 0:32].rearrange("p (r d) -> p r d", r=2).unsqueeze(1).to_broadcast([128, H, 2, 16])
        sin = TABA[:, t, 32:64].rearrange("p (r d) -> p r d", r=2).unsqueeze(1).to_broadcast([128, H, 2, 16])
        tt("pool", a5[:, :, :, 0, :], x5[:, :, :, 0, :], cos, ALU.mult, R, W)
        tt("pool", a5[:, :, :, 1, :], x5[:, :, :, 1, :], cos, ALU.mult, R, W)
        tt("pool", b5[:, :, :, 0, :], x5[:, :, :, 1, :], sin, ALU.mult, R, W)
        tt("pool", b5[:, :, :, 1, :], x5[:, :, :, 0, :], sin, ALU.mult, R, W)
        o5 = QKB[:, sl, 0:H * 64].rearrange("p (h r f d) -> p h r f d", h=H, r=2, f=2)
        tt("dve", o5[:, :, :, 0, :], a5[:, :, :, 0, :], b5[:, :, :, 0, :], ALU.subtract, [RTb[sl]], [QKBb[sl]])
        tt("dve", o5[:, :, :, 1, :], a5[:, :, :, 1, :], b5[:, :, :, 1, :], ALU.add, [RTb[sl]], [QKBb[sl]])

    def proj_token_major(l, g, c0, kinds, smp):
        slot = wreq(slab256(w_in, l, c0, permq=(g in ("A", "B") and kinds[0][0] == "q")))
        wv = WS[:, slot, :].rearrange("p (k n) -> p k n", k=KC)
        for t in range(NT):
            fi = next_fb(0, 4)
            for kc in range(KC):
                mm(FB[fi][:, 0:256], XNT[:, kc, 1 + t * 128:1 + (t + 1) * 128], wv[:, kc, :], kc == 0, kc == KC - 1,
                   [XNTb[t], WSb[slot]], [Fb[fi]])
            off = 0
            for kind, ncol in kinds:
                if kind == "v":
                    nh = ncol // 64
                    dst = VE[:, 8 + t, 0:nh * 65].rearrange("p (h w) -> p h w", h=nh)[:, :, 0:64]
                    src = FB[fi][:, off:off + ncol].rearrange("p (h d) -> p h d", h=nh)
                    act(dst, src, AF.Copy, [Fb[fi]], [VEb[8 + t]])
                else:
                    H = ncol // 64
                    sl = t % 2
                    act(TM[:, sl, 0:ncol], FB[fi][:, off:off + ncol], AF.Copy, [Fb[fi]], [TMb[sl]])
                    perm = (g in ("A", "B")) and kind == "q"
                    if g == "A":
                        rope_a(sl, t, H, perm, 0 if kind == "q" else 1)
                    else:
                        rope_bc(sl, t, H, perm)
                    npair = ncol // 128
                    for pr in range(npair):
                        tr(T0[:, pr * 128:(pr + 1) * 128], QKB[:, sl, pr * 128:(pr + 1) * 128], [QKBb[sl]], [T0b[0]])
                    src = T0[:, 0:npair * 128].rearrange("p (a t) -> p a t", a=npair)
                    if kind == "q":
                        OP("dve", lambda e, src=src, t=t, npair=npair: e.tensor_copy(out=QT[:, 0:npair, t * 128:(t + 1) * 128], in_=src),
                           [T0b[0]], [QTb[t]])
                    else:
                        OP("dve", lambda e, src=src, t=t, npair=npair: e.tensor_copy(out=KT[:, 0:npair, (8 + t) * 128:(9 + t) * 128], in_=src),
                           [T0b[0]], [KTb[8 + t]])
                off += ncol

    def proj_feature_major(l, c0, which):
        slot = wreq(slab256(w_in, l, c0))
        wv = WS[:, slot, :].rearrange("p (k n) -> p k n", k=KC)
        for pr in range(2):
            for tb in range(4):
                fi = next_fb(0, 4)
                for kc in range(KC):
                    mm(FB[fi][:, 0:512], wv[:, kc, pr * 128:(pr + 1) * 128], XNT[:, kc, 1 + tb * 512:1 + (tb + 1) * 512],
                       kc == 0, kc == KC - 1, [XNTb[4 * tb + i] for i in range(4)] + [WSb[slot]], [Fb[fi]])
                if which == "q":
                    act(QT[:, pr, tb * 512:(tb + 1) * 512], FB[fi][:, 0:512], AF.Copy, [Fb[fi]], [QTb[4 * tb + i] for i in range(4)])
                else:
                    act(KT[:, pr, 1024 + tb * 512:1024 + (tb + 1) * 512], FB[fi][:, 0:512], AF.Copy, [Fb[fi]],
                        [KTb[8 + 4 * tb + i] for i in range(4)])

    def exchange(l, g, halo):
        npr, vw = GW[g]
        pk, gt = pack_d[(l, g)], gath_d[(l, g)]
        key = (l, g)
        own = list(range(8, 24))
        dma("sp", pk[:, 0:npr * T].rearrange("p (a t) -> p a t", a=npr), KT[:, 0:npr, 1024:3072],
            [KTb[j] for j in own], [PACKb[key]], d_pack)
        dma("sp", pk[:, npr * T:npr * T + NT * vw].rearrange("p (j w) -> p j w", j=NT), VE[:, 8:24, 0:vw],
            [VEb[j] for j in own], [PACKb[key]], d_pack)
        OP("pool", lambda e: e.collective_compute("AllGather", ALU.bypass, replica_groups=[list(range(NCORES))],
                                                   ins=[pk[:, :]], outs=[gt[:, :]]),
           [PACKb[key]], [GATHb[key]], dsem=d_cc, inc=-1)
        if halo == 0:
            return
        h = halo
        for side in range(2):
            def rows_fn(side=side):
                rowv = regs["left"] if side == 0 else regs["right"]
                return gt[bass.DynSlice(rowv, 128), :]
            if side == 0:
                ksrc = lambda rf=rows_fn: rf()[:, 0:npr * T].rearrange("p (a t) -> p a t", a=npr)[:, :, T - h * 128:T]
                kdst = KT[:, 0:npr, (8 - h) * 128:1024]
                vsrc = lambda rf=rows_fn: rf()[:, npr * T:npr * T + NT * vw].rearrange("p (j w) -> p j w", j=NT)[:, NT - h:NT, :]
                vdst = VE[:, 8 - h:8, 0:vw]
                js = list(range(8 - h, 8))
            else:
                ksrc = lambda rf=rows_fn: rf()[:, 0:npr * T].rearrange("p (a t) -> p a t", a=npr)[:, :, 0:h * 128]
                kdst = KT[:, 0:npr, 3072:3072 + h * 128]
                vsrc = lambda rf=rows_fn: rf()[:, npr * T:npr * T + NT * vw].rearrange("p (j w) -> p j w", j=NT)[:, 0:h, :]
                vdst = VE[:, 24:24 + h, 0:vw]
                js = list(range(24, 24 + h))
            OP("sp", lambda e, kdst=kdst, ksrc=ksrc: e.dma_start(out=kdst, in_=ksrc()), [GATHb[key]], [KTb[j] for j in js], dsem=d_kv[side])
            OP("sp", lambda e, vdst=vdst, vsrc=vsrc: e.dma_start(out=vdst, in_=vsrc()), [GATHb[key]], [VEb[j] for j in js], dsem=d_kv[side])
            ts("pool", vdst, vdst, CVAL[:, side:side + 1], None, ALU.mult, None, [VEb[j] for j in js] + [CONSTb],
               [VEb[j] for j in js])

    def attention(l, g, smp):
        gqa = g in ("A", "B")
        npr, vw = GW[g]
        QB = 2 if g == "A" else 1
        nblk = 512 // (QB * 128)
        gidx = "ABCD".index(g)
        stream = smp and g == "A"

        def heads_of(pr):
            if gqa:
                return (pr, pr + 2), (0, 1)
            return (2 * pr, 2 * pr + 1), (2 * pr, 2 * pr + 1)

        def key_tiles(i):
            if g == "A":
                return list(range(NT)), None
            if g == "B":
                lo, hi, moff, d0 = i - 1, i + 1, 0, -1
            elif g == "C":
                lo, hi, moff, d0 = i - 8, i + 8, 0, -8
            else:
                case = {0: "q0", 1: "q1", 14: "q14", 15: "q15"}.get(i, "int")
                dl = dict(D_CASES)[case]
                lo, hi, moff, d0 = i + dl[0], i + dl[-1], D_OFF[case], dl[0]
            js = [j for j in range(lo, hi + 1) if smp or 0 <= j < NT]
            return js, (moff, d0)

        for qb in range(NT // QB):
            qts = [qb * QB + a for a in range(QB)]
            accs = [4 + a for a in range(QB)]
            first_acc = {a: True for a in accs}
            if stream:
                chunks = list(range(NCORES))
            else:
                chunks = [None]
            for ch in chunks:
                if stream:
                    sidx = ch % 2
                    gt = gath_d[(l, g)]
                    rows = gt[ch * 128:(ch + 1) * 128, :]
                    jj = [sidx * 16 + i for i in range(NT)]
                    dma("sp", KT[:, 0, sidx * T:(sidx + 1) * T], rows[:, 0:T], [GATHb[(l, g)]], [KTb[j] for j in jj], d_kv[sidx])
                    dma("sp", VE[:, sidx * 16:(sidx + 1) * 16, 0:vw], rows[:, T:T + NT * vw].rearrange("p (j w) -> p j w", j=NT),
                        [GATHb[(l, g)]], [VEb[j] for j in jj], d_kv[sidx])
                    kslots = [(sidx * 16 + i, None) for i in range(NT)]
                else:
                    js, minfo = key_tiles(qts[0])
                    kslots = [(8 + j, (minfo[0] + (j - qts[0]) - minfo[1]) if minfo else None) for j in js]
                groups = [kslots[a:a + nblk] for a in range(0, len(kslots), nblk)]
                work = [(pr, grp) for grp in groups for pr in range(2)]
                pend = None
                for wi, (pr, grp) in enumerate(work):
                    (hlo, hhi), (klo, khi) = heads_of(pr)
                    sb_i = wi % 2
                    banks = (0 + 2 * sb_i, 1 + 2 * sb_i)
                    pts = (0 + 2 * sb_i, 1 + 2 * sb_i)
                    kpair = 0 if gqa else pr
                    qcols = (qts[0] * 128, (qts[-1] + 1) * 128)
                    qn = QB * 128
                    for bi, (slot_j, mi) in enumerate(grp):
                        for half, base in ((0, 0), (1, 64)):
                            mm(FB[banks[half]][:, bi * qn:(bi + 1) * qn],
                               KT[base:base + 64, kpair, slot_j * 128:(slot_j + 1) * 128],
                               QT[base:base + 64, pr, qcols[0]:qcols[1]], True, True,
                               [KTb[slot_j]] + [QTb[q] for q in qts], [Fb[banks[half]]])
                    ncol = len(grp) * qn
                    for half in range(2):
                        act(PT[:, pts[half], 0:ncol], FB[banks[half]][:, 0:ncol], AF.Exp, [Fb[banks[half]]], [PTb[pts[half]]],
                            scale=0.125)
                        if g in ("B", "C"):
                            m0 = grp[0][1]
                            msk = (MB if g == "B" else MC)[:, m0 * 128:(m0 + len(grp)) * 128]
                            tt("pool", PT[:, pts[half], 0:ncol], PT[:, pts[half], 0:ncol], msk, ALU.mult,
                               [PTb[pts[half]], MDb], [PTb[pts[half]]])
                        elif g == "D":
                            m0 = grp[0][1]
                            hq = (hlo, hhi)[half]
                            i0 = qts[0]
                            case = {0: "q0", 1: "q1", 14: "q14", 15: "q15"}.get(i0, "int")
                            d0 = dict(D_CASES)[case][0]
                            dfirst = d0 + (m0 - D_OFF[case])
                            eb = EXPB[:, hq, dfirst + 3:dfirst + 3 + len(grp), :].rearrange("p d q -> p (d q)")
                            tt("pool", PT[:, pts[half], 0:ncol], PT[:, pts[half], 0:ncol], eb, ALU.mult,
                               [PTb[pts[half]], EXPBb], [PTb[pts[half]]])
                            tt("pool", PT[:, pts[half], 0:ncol], PT[:, pts[half], 0:ncol], MD01[:, m0 * 128:(m0 + len(grp)) * 128],
                               ALU.mult, [PTb[pts[half]], MDb], [PTb[pts[half]]])
                    if pend is not None:
                        emit_pv(*pend)
                    pend = (grp, pts, (hlo, hhi), (klo, khi), accs, first_acc, QB, qn, vw)
                if pend is not None:
                    emit_pv(*pend)
            for a, qt in enumerate(qts):
                epilogue(l, g, gidx, qt, accs[a])

    def emit_pv(grp, pts, hq, hk, accs, first_acc, QB, qn, vw):
        for bi, (slot_j, mi) in enumerate(grp):
            for half in range(2):
                h, kvh = hq[half], hk[half]
                for a in range(QB):
                    acc = accs[a]
                    st = first_acc[acc]
                    first_acc[acc] = False
                    mm(FB[acc][:, h * 65:(h + 1) * 65],
                       PT[:, pts[half], bi * qn + a * 128:bi * qn + (a + 1) * 128],
                       VE[:, slot_j, kvh * 65:(kvh + 1) * 65], st, True,
                       [PTb[pts[half]], VEb[slot_j]], [Fb[acc]])

    def epilogue(l, g, gidx, qt, acc):
        a3 = FB[acc][:, 0:260].rearrange("p (h w) -> p h w", h=4)
        if g == "B":
            tt("dve", SM[:, 24:28], a3[:, :, 64], ESINK[:, 0:4], ALU.add, [Fb[acc], SINKb], [STATb])
            OP("dve", lambda e: e.reciprocal(out=SM[:, 28:32], in_=SM[:, 24:28]), [STATb], [STATb])
        else:
            OP("dve", lambda e: e.reciprocal(out=SM[:, 28:32], in_=a3[:, :, 64]), [Fb[acc]], [STATb])
        y3 = YF.rearrange("p (h d) -> p h d", h=4)
        tt("dve", y3, a3[:, :, 0:64], SM[:, 28:32].unsqueeze(2).to_broadcast([128, 4, 64]), ALU.mult, [Fb[acc], STATb], [YFb])
        stt(SQJ[:, 0:256], YF, 1.0, YF, ALU.mult, ALU.mult, [YFb], [STATb], accum_out=SM[:, 32:33])
        act(SM[:, 33:34], SM[:, 32:33], AF.Ln, [STATb], [STATb], scale=1.0 / 256, bias=EPS)
        act(SM[:, 34:35], SM[:, 33:34], AF.Exp, [STATb], [STATb], scale=-0.5)
        stt(YB, YF, SM[:, 34:35], GGRP[:], ALU.mult, ALU.mult, [YFb, STATb, GGRPb], [YBb])
        ys = qt % 2
        for kc in range(2):
            tr(T0[:, 512 + kc * 128:512 + (kc + 1) * 128], YB[:, kc * 128:(kc + 1) * 128], [YBb], [T0b[1]])
        act(YT[:, ys, :, :], T0[:, 512:768].rearrange("p (k t) -> p k t", k=2), AF.Copy, [T0b[1]], [YTb[ys]])
        for hf in range(2):
            for kc in range(2):
                mm(FB[6][:, 0:512], YT[:, ys, kc, :], WO[:, kc, hf * 512:(hf + 1) * 512], kc == 0, kc == 1,
                   [YTb[ys], WOb], [Fb[6]])
            tt("dve", X[:, qt, hf * 512:(hf + 1) * 512], X[:, qt, hf * 512:(hf + 1) * 512], FB[6][:, 0:512], ALU.add,
               [Fb[6], Xb[qt]], [Xb[qt]])

    def mixer(s, l, smp):
        stop("loadx")
        load_seq_tables(smp)
        norm_to_T(lambda t: X[:, t, :], NT, Xb, norm_mix_g[l:l + 1, :], lambda t: XNT[:, :, 1 + t * 128:1 + (t + 1) * 128], XNTb)
        stop("norm")
        dma("sp", QKG[:, 0, :], aqg[l:l + 1, :].partition_broadcast(128), [], [QKGb], d_qkg)
        dma("sp", QKG[:, 1, :], akg[l:l + 1, :].partition_broadcast(128), [], [QKGb], d_qkg)
        dma("sp", SINK[:], bsink[l:l + 1, :].partition_broadcast(128), [], [SINKb], d_sink)
        act(ESINK[:], SINK[:], AF.Exp, [SINKb], [SINKb])
        dma("sp", EXPB.rearrange("p h d q -> p (h d q)"), expb_d[l, :, :], [], [EXPBb], d_expb)
        for g in "ABCD":
            gi = "ABCD".index(g)
            nv = 2 if g in ("A", "B") else 4
            OP("pool", lambda e, nv=nv: e.memset(VE[:, 8:24, 0:nv * 65].rearrange("p j (h w) -> p j h w", h=nv)[:, :, :, 64:65], 1.0),
               [], [VEb[j] for j in range(8, 24)])
            c0 = gi * 512 if gi < 2 else 1024 + (gi - 2) * 768
            if g in ("A", "B"):
                proj_token_major(l, g, c0, [("q", 256)], smp)
                proj_token_major(l, g, c0 + 256, [("k", 128), ("v", 128)], smp)
            elif g == "C":
                proj_token_major(l, g, c0, [("q", 256)], smp)
                proj_token_major(l, g, c0 + 256, [("k", 256)], smp)
                proj_token_major(l, g, c0 + 512, [("v", 256)], smp)
            else:
                proj_feature_major(l, c0, "q")
                proj_feature_major(l, c0 + 256, "k")
                proj_token_major(l, g, c0 + 512, [("v", 256)], smp)
            dma("sp", GGRP[:], grpg[l:l + 1, gi * 256:(gi + 1) * 256].partition_broadcast(128), [], [GGRPb], d_ggrp)
            dma("pool", WO, w_out[l, gi * 256:(gi + 1) * 256, :].rearrange("(k p) n -> p k n", p=128), [], [WOb], d_wo)
            stop("proj" + g)
            if smp:
                exchange(l, g, {"A": 0, "B": 1, "C": 8, "D": 2}[g])
            stop("exch" + g)
            attention(l, g, smp)
            stop("attn" + g)

    def cross(s, l, smp):
        dma("sp", MEMX, mem[s, :, :].rearrange("(j p) n -> p j n", p=128), [], [MEMXb], d_mem)
        norm_to_T(lambda j: MEMX[:, j, :], 2, [MEMXb, MEMXb], norm_mem_g[l:l + 1, :],
                  lambda j: MNT[:, :, j * 128:(j + 1) * 128], [MNTb, MNTb])
        for sl4 in range(4):
            slot = wreq(slab256(w_xkv, l, sl4 * 256))
            wv = WS[:, slot, :].rearrange("p (k n) -> p k n", k=KC)
            for j in range(2):
                fi = next_fb(0, 4)
                for kc in range(KC):
                    mm(FB[fi][:, 0:256], wv[:, kc, j * 128:(j + 1) * 128], MNT[:, kc, :], kc == 0, kc == KC - 1,
                       [MNTb, WSb[slot]], [Fb[fi]])
                act(MK[:, 2 * sl4 + j, :], FB[fi][:, 0:256], AF.Copy, [Fb[fi]], [MKb])
        for sl4 in range(4):
            slot = wreq(slab256(w_xkv, l, D + sl4 * 256))
            wv = WS[:, slot, :].rearrange("p (k n) -> p k n", k=KC)
            for mt in range(2):
                fi = next_fb(0, 4)
                for kc in range(KC):
                    mm(FB[fi][:, 0:256], MNT[:, kc, mt * 128:(mt + 1) * 128], wv[:, kc, :], kc == 0, kc == KC - 1,
                       [MNTb, WSb[slot]], [Fb[fi]])
                act(MV[:, mt, sl4 * 256:(sl4 + 1) * 256], FB[fi][:, 0:256], AF.Copy, [Fb[fi]], [MVb])
        norm_to_T(lambda t: X[:, t, :], NT, Xb, norm_x_g[l:l + 1, :], lambda t: XNT[:, :, 1 + t * 128:1 + (t + 1) * 128], XNTb)
        for sl4 in range(4):
            slot = wreq(slab256(w_xq, l, sl4 * 256))
            wv = WS[:, slot, :].rearrange("p (k n) -> p k n", k=KC)
            for j in range(2):
                for tb in range(4):
                    fi = next_fb(0, 2)
                    for kc in range(KC):
                        mm(FB[fi][:, 0:512], wv[:, kc, j * 128:(j + 1) * 128], XNT[:, kc, 1 + tb * 512:1 + (tb + 1) * 512],
                           kc == 0, kc == KC - 1, [XNTb[4 * tb + i] for i in range(4)] + [WSb[slot]], [Fb[fi]])
                    act(QX[:, 2 * sl4 + j, tb * 512:(tb + 1) * 512], FB[fi][:, 0:512], AF.Copy, [Fb[fi]], [QXb[tb]])
        for tb in range(4):
            for h in range(4):
                for mt in range(2):
                    for dc in range(2):
                        mm(FB[2 + mt][:, 0:512], MK[:, 2 * h + dc, mt * 128:(mt + 1) * 128], QX[:, 2 * h + dc, tb * 512:(tb + 1) * 512],
                           dc == 0, dc == 1, [MKb, QXb[tb]], [Fb[2 + mt]])
                    pi = (h % 2) * 2 + mt
                    act(PTX[:, pi, :], FB[2 + mt][:, 0:512], AF.Exp, [Fb[2 + mt]], [PTb[pi]], scale=1.0 / 16)
                for mt in range(2):
                    pi = (h % 2) * 2 + mt
                    mm(FB[4][:, 0:512], ONES[:], PTX[:, pi, :], mt == 0, mt == 1, [PTb[pi], CONSTb], [Fb[4]])
                for dc in range(2):
                    for mt in range(2):
                        pi = (h % 2) * 2 + mt
                        mm(FB[5 + dc][:, 0:512], MV[:, mt, (2 * h + dc) * 128:(2 * h + dc + 1) * 128], PTX[:, pi, :],
                           mt == 0, mt == 1, [PTb[pi], MVb], [Fb[5 + dc]])
                dr = h % 2
                OP("dve", lambda e, dr=dr: e.reciprocal(out=DENR[:, dr, :], in_=FB[4][:, 0:512]), [Fb[4]], [DENRb[dr]])
                for dc in range(2):
                    tt("dve", XNT[:, 2 * h + dc, 1 + tb * 512:1 + (tb + 1) * 512], FB[5 + dc][:, 0:512], DENR[:, dr, :], ALU.mult,
                       [Fb[5 + dc], DENRb[dr]], [XNTb[4 * tb + i] for i in range(4)])
        for sl4 in range(4):
            slot = wreq(slab256(w_xo, l, sl4 * 256))
            wv = WS[:, slot, :].rearrange("p (k n) -> p k n", k=KC)
            for t in range(NT):
                fi = next_fb(0, 4)
                for kc in range(KC):
                    mm(FB[fi][:, 0:256], XNT[:, kc, 1 + t * 128:1 + (t + 1) * 128], wv[:, kc, :], kc == 0, kc == KC - 1,
                       [XNTb[t], WSb[slot]], [Fb[fi]])
                tt("dve", X[:, t, sl4 * 256:(sl4 + 1) * 256], X[:, t, sl4 * 256:(sl4 + 1) * 256], FB[fi][:, 0:256], ALU.add,
                   [Fb[fi], Xb[t]], [Xb[t]])

    def ffn(s, l, smp):
        norm_to_T(lambda t: X[:, t, :], NT, Xb, norm_ffn_g[l:l + 1, :], lambda t: XNT[:, :, 1 + t * 128:1 + (t + 1) * 128], XNTb)
        if smp:
            key = (l, "F")
            pk, gt = pack_d[key], gath_d[key]
            dma("sp", pk[:, 0:8], XNT[:, :, 1], [XNTb[0]], [PACKb[key]], d_pack, allow_slow_non_contiguous=True)
            dma("sp", pk[:, 8:16], XNT[:, :, T], [XNTb[NT - 1]], [PACKb[key]], d_pack, allow_slow_non_contiguous=True)
            OP("pool", lambda e: e.collective_compute("AllGather", ALU.bypass, replica_groups=[list(range(NCORES))],
                                                       ins=[pk[:, :]], outs=[gt[:, :]]),
               [PACKb[key]], [GATHb[key]], dsem=d_cc, inc=-1)
            OP("sp", lambda e: e.dma_start(out=XNT[:, :, 0], in_=gt[bass.DynSlice(regs["left"], 128), 8:16],
                                           allow_slow_non_contiguous=True), [GATHb[key]], [XNTh], dsem=d_kv[0])
            OP("sp", lambda e: e.dma_start(out=XNT[:, :, T + 1], in_=gt[bass.DynSlice(regs["right"], 128), 0:8],
                                           allow_slow_non_contiguous=True), [GATHb[key]], [XNTh], dsem=d_kv[1])
            ts("pool", XNT[:, :, 0], XNT[:, :, 0], CVAL[:, 0:1], None, ALU.mult, None, [XNTh, CONSTb], [XNTh])
            ts("pool", XNT[:, :, T + 1], XNT[:, :, T + 1], CVAL[:, 1:2], None, ALU.mult, None, [XNTh, CONSTb], [XNTh])
        else:
            OP("pool", lambda e: e.memset(XNT[:, :, 0:1], 0.0), [], [XNTh])
            OP("pool", lambda e: e.memset(XNT[:, :, T + 1:T + 2], 0.0), [], [XNTh])
        wd_i = {"i": 0}

        def wd_load(nb):
            slot = wd_i["i"] % 2
            wd_i["i"] += 1
            dma("pool", WD[:, slot, :, :], w_down[l, :, nb * 256:(nb + 1) * 256].rearrange("(f p) n -> p f n", p=128),
                [], [WDb[slot]], d_wd[slot])
            return slot

        for hf in range(2):
            tiles = list(range(hf * 8, hf * 8 + 8))
            gtiles = list(range(max(0, hf * 8 - 1), min(NT, hf * 8 + 9)))
            wd_load(0)
            wd_load(1)
            for f in range(NFC):
                parts = [(lambda slot: WS[:, slot, 0:1024].rearrange("p (k n) -> p k n", k=KC),
                          w_up[l, :, f * 128:(f + 1) * 128].rearrange("(k p) n -> p k n", p=128)),
                         (lambda slot: WS[:, slot, 1024:2048].rearrange("p (k n) -> p k n", k=KC),
                          w_up[l, :, DFF + f * 128:DFF + (f + 1) * 128].rearrange("(k p) n -> p k n", p=128))]
                slot = wreq(parts)
                wg = WS[:, slot, 0:1024].rearrange("p (k n) -> p k n", k=KC)
                wvv = WS[:, slot, 1024:2048].rearrange("p (k n) -> p k n", k=KC)
                gs = f % 2
                for gi3 in range(3):
                    c0 = hf * 1024 + gi3 * 342
                    for kc in range(KC):
                        mm(FB[gi3][:, 0:342], wg[:, kc, :], XNT[:, kc, c0:c0 + 342], kc == 0, kc == KC - 1,
                           [XNTb[t] for t in gtiles] + [XNTh, WSb[slot]], [Fb[gi3]])
                    act(GST[:, gs, gi3 * 342:(gi3 + 1) * 342], FB[gi3][:, 0:342], AF.Copy, [Fb[gi3]], [GSTb[gs]])
                vb = 3 + 2 * (f % 2)
                for vi in range(2):
                    c0 = hf * 1024 + 1 + vi * 512
                    for kc in range(KC):
                        mm(FB[vb + vi][:, 0:512], wvv[:, kc, :], XNT[:, kc, c0:c0 + 512], kc == 0, kc == KC - 1,
                           [XNTb[t] for t in tiles] + [WSb[slot]], [Fb[vb + vi]])
                ts("dve", CT, GST[:, gs, 1:1025], CW[:, l, f, 1:2], CW[:, l, f, 3:4], ALU.mult, ALU.add, [GSTb[gs], CONSTb], [CTb])
                stt(CT, GST[:, gs, 0:1024], CW[:, l, f, 0:1], CT, ALU.mult, ALU.add, [GSTb[gs], CONSTb, CTb], [CTb])
                stt(CT, GST[:, gs, 2:1026], CW[:, l, f, 2:3], CT, ALU.mult, ALU.add, [GSTb[gs], CONSTb, CTb], [CTb])
                act(UB, CT, AF.Gelu, [CTb], [UBb])
                for vi in range(2):
                    tt("dve", HT[:, f, vi * 512:(vi + 1) * 512], UB[:, vi * 512:(vi + 1) * 512], FB[vb + vi][:, 0:512], ALU.mult,
                       [UBb, Fb[vb + vi]], [HTb[f]])
            for nb in range(4):
                slot = nb % 2
                for tl in range(8):
                    fi = next_fb(0, 3)
                    for f in range(NFC):
                        mm(FB[fi][:, 0:256], HT[:, f, tl * 128:(tl + 1) * 128], WD[:, slot, f, :], f == 0, f == NFC - 1,
                           [HTb[f], WDb[slot]], [Fb[fi]])
                    t = hf * 8 + tl
                    tt("dve", X[:, t, nb * 256:(nb + 1) * 256], X[:, t, nb * 256:(nb + 1) * 256], FB[fi][:, 0:256], ALU.add,
                       [Fb[fi], Xb[t]], [Xb[t]])
                if nb + 2 < 4:
                    wd_load(nb + 2)

    def final_out(s):
        dma("sp", GBC[:], final_g[0:1, :].partition_broadcast(128), [], [GBCb], d_gbc)
        norm_stats(lambda t: X[:, t, :], NT, Xb)
        for t in range(NT):
            sl = t % 2
            stt(OUTS[:, sl, :], X[:, t, :], RSTD[:, t:t + 1], GBC[:], ALU.mult, ALU.mult, [Xb[t], STATb, GBCb], [OUTSb[sl]])
            dma("sp", yout[s, t * 128:(t + 1) * 128, :], OUTS[:, sl, :], [OUTSb[sl]], [], d_out[sl])

    def gen():
        wstate["i"] = 0
        fbrr["i"] = 0
        load_consts()
        stop("consts")
        prev_smp = None
        for s in range(SEQ0, SEQ0 + NSEQ):
            smp = (s == 4)
            if smp != prev_smp:
                S.barrier()
                prev_smp = smp
            load_x(s)
            for l in range(NLAYER):
                mixer(s, l, smp)
                S.barrier()
                stop("mixer")
                cross(s, l, smp)
                stop("cross")
                S.barrier()
                ffn(s, l, smp)
                stop("ffn")
                S.barrier()
            final_out(s)
            S.barrier()

    S.dry = True
    gen()
    S.dry = False
    gen()
    S.final_wait("sp")
    block = es.enter_context(nc.Block())
    regsem = sem("regsem")

    @block.sync
    def _(e):
        e.dma_start(out=CINF[:], in_=cinfo_d[:, :]).then_inc(regsem, 16)
        e.wait_ge(regsem, 16)
        rl = e.register("rleft").__enter__()
        rr = e.register("rright").__enter__()
        e.reg_load(rl, CINF[0:1, 0:1])
        e.reg_load(rr, CINF[0:1, 1:2])
        regs["left"] = e.snap(rl, min_val=0, max_val=NCORES - 1) * 128
        regs["right"] = e.snap(rr, min_val=0, max_val=NCORES - 1) * 128
        replay(e, S.engs["sp"].prog)

    @block.tensor
    def _(e):
        replay(e, S.engs["pe"].prog)

    @block.scalar
    def _(e):
        replay(e, S.engs["act"].prog)

    @block.vector
    def _(e):
        replay(e, S.engs["dve"].prog)

    @block.gpsimd
    def _(e):
        replay(e, S.engs["pool"].prog)

    es.close()
    ninstr = {k: len(v.prog) for k, v in S.engs.items()}
    return nc, ninstr


_CACHE = {}


def _f32(a):
    return np.ascontiguousarray(np.asarray(a, dtype=np.float32))


def kernel(x_prompt, x_sample, mem_prompt, mem_sample, norm_mix_g, w_in, a_q_norm_g, a_k_norm_g, b_sink, d_rpb,
           grp_norm_g, w_out, norm_x_g, norm_mem_g, w_xq, w_xkv, w_xo, norm_ffn_g, w_up, conv_w, conv_b, w_down,
           final_norm_g):
    if "nc" not in _CACHE:
        _CACHE["nc"] = build_program()
    nc, ninstr = _CACHE["nc"]
    x_prompt, x_sample, mem_prompt, mem_sample = map(_f32, (x_prompt, x_sample, mem_prompt, mem_sample))
    shared = dict(
        norm_mix_g=_f32(norm_mix_g), w_in=_f32(w_in), aqg=_f32(a_q_norm_g), akg=_f32(a_k_norm_g), bsink=_f32(b_sink),
        grpg=_f32(grp_norm_g).reshape(4, D), w_out=_f32(w_out), norm_x_g=_f32(norm_x_g), norm_mem_g=_f32(norm_mem_g),
        w_xq=_f32(w_xq), w_xkv=_f32(w_xkv), w_xo=_f32(w_xo), norm_ffn_g=_f32(norm_ffn_g), w_up=_f32(w_up),
        conv_w=_f32(conv_w), conv_b=_f32(conv_b), w_down=_f32(w_down), final_g=_f32(final_norm_g).reshape(1, D),
        rpbT=_rpb_toeplitz(_f32(d_rpb)), ident=np.eye(128, dtype=np.float32), maskB=_mask_b(), maskC=_mask_c(),
    )
    tb0, ta0 = _tables(0)
    md0 = _mask_d(0, T)
    in_maps = []
    for c in range(NCORES):
        tb1, ta1 = _tables(T * c)
        m = dict(shared)
        m["xin"] = np.ascontiguousarray(np.concatenate([x_prompt[4 * c:4 * c + 4], x_sample[0, T * c:T * (c + 1)][None]], axis=0))
        m["mem"] = np.ascontiguousarray(np.concatenate([mem_prompt[4 * c:4 * c + 4], mem_sample[0:1]], axis=0))
        m["maskD"] = np.ascontiguousarray(np.stack([md0, _mask_d(T * c, T * NCORES)]))
        m["tabBC"] = np.ascontiguousarray(np.stack([tb0, tb1]))
        m["tabA"] = np.ascontiguousarray(np.stack([ta0, ta1]))
        m["cvalid"] = np.ascontiguousarray(np.tile(np.array([[1.0 if c > 0 else 0.0, 1.0 if c < NCORES - 1 else 0.0]], np.float32), (128, 1)))
        m["cinfo"] = np.array([[max(c - 1, 0), min(c + 1, NCORES - 1)]], np.int32)
        in_maps.append(m)
    res = run_bass_kernel_spmd(nc, in_maps, core_ids=list(range(NCORES)))
    y_prompt = np.empty((32, T, D), np.float32)
    y_sample = np.empty((1, T * NCORES, D), np.float32)
    for c in range(NCORES):
        yo = res.results[c]["yout"]
        y_prompt[4 * c:4 * c + 4] = yo[0:4]
        y_sample[0, T * c:T * (c + 1)] = yo[4]
    return y_prompt, y_sample
```

```python
import os
from contextlib import ExitStack
import numpy as np
import concourse.bass as bass
import concourse.mybir as mybir
from concourse.bass_utils import run_bass_kernel_spmd

F32 = mybir.dt.float32
BF16 = mybir.dt.bfloat16
I32 = mybir.dt.int32
AF = mybir.ActivationFunctionType
ALU = mybir.AluOpType
AX = mybir.AxisListType

NCORES = 8
T = 2048
NT = 16
D = 1024
KC = 8
DFF = 2816
NFC = 22
EPS = 1e-6
NSEQ = int(os.environ.get("MK_NSEQ", "5"))
NLAYER = int(os.environ.get("MK_NLAYER", "4"))
SEQ0 = int(os.environ.get("MK_SEQ0", "0"))


class Tok:
    __slots__ = ("sem", "val", "eng")

    def __init__(self, sem, val, eng):
        self.sem, self.val, self.eng = sem, val, eng


class Buf:
    __slots__ = ("name", "w", "r", "excl")

    def __init__(self, name, excl=False):
        self.name, self.w, self.r, self.excl = name, None, {}, excl


class DSem:
    def __init__(self, sem):
        self.sem, self.count = sem, 0


class Eng:
    def __init__(self, name, sem):
        self.name, self.sem, self.count, self.waited, self.prog = name, sem, 0, {}, []


class SBoard:
    def __init__(self):
        self.engs = {}
        self.dsems = []
        self.dry = False

    def add_engine(self, name, sem):
        self.engs[name] = Eng(name, sem)

    def op(self, eng, fn, reads=(), writes=(), dsem=None, inc=None):
        if self.dry:
            return None
        e = self.engs[eng]
        deps = []
        ex = [b for b in reads if b.excl]
        if ex:
            reads = [b for b in reads if not b.excl]
            writes = list(writes) + [b for b in ex if b not in writes]
        for b in reads:
            if b.w is not None:
                deps.append(b.w)
        for b in writes:
            if b.w is not None:
                deps.append(b.w)
            deps.extend(b.r.values())
        for tk in deps:
            if tk.eng == "pe" and eng == "pe":
                continue
            key = id(tk.sem)
            if e.waited.get(key, 0) >= tk.val:
                continue
            e.waited[key] = tk.val
            e.prog.append(("w", tk.sem, tk.val))
        if dsem is None:
            e.count += 1
            tok = Tok(e.sem, e.count, eng)
            e.prog.append(("i", fn, e.sem, 1))
        else:
            step = 16 if inc is None else inc
            dsem.count += (1 if step < 0 else step)
            tok = Tok(dsem.sem, dsem.count, "dma")
            e.prog.append(("i", fn, dsem.sem, step))
        for b in reads:
            b.r[id(tok.sem)] = tok
        for b in writes:
            b.w = tok
            b.r = {}
        return tok

    def barrier(self):
        if self.dry:
            return
        toks = [Tok(e.sem, e.count, e.name) for e in self.engs.values() if e.count > 0]
        toks += [Tok(d.sem, d.count, "dma") for d in self.dsems if d.count > 0]
        for e in self.engs.values():
            for tk in toks:
                if tk.sem is e.sem:
                    continue
                key = id(tk.sem)
                if e.waited.get(key, 0) >= tk.val:
                    continue
                e.waited[key] = tk.val
                e.prog.append(("w", tk.sem, tk.val))

    def final_wait(self, eng):
        e = self.engs[eng]
        for d in self.dsems:
            if d.count > 0 and e.waited.get(id(d.sem), 0) < d.count:
                e.prog.append(("w", d.sem, d.count))
        for o in self.engs.values():
            if o is not e and o.count > 0:
                e.prog.append(("w", o.sem, o.count))


def replay(eng_handle, prog):
    for it in prog:
        if it[0] == "w":
            eng_handle.wait_ge(it[1], it[2])
        else:
            ins = it[1](eng_handle)
            if it[3] < 0:
                ins.then_inc(it[2])
            else:
                ins.then_inc(it[2], it[3])


def _rope_tables(pos, half):
    freqs = (np.float32(10000.0) ** (-np.arange(half, dtype=np.float32) / np.float32(half))).astype(np.float32)
    ang = (pos.astype(np.float32)[:, None] * freqs[None, :]).astype(np.float32)
    return np.cos(ang).astype(np.float32), np.sin(ang).astype(np.float32)


def _tile_layout(a):
    n = a.shape[1]
    return np.ascontiguousarray(a.reshape(NT, 128, n).transpose(1, 0, 2).reshape(128, NT * n))


def _tables(tok0):
    pos = np.arange(T, dtype=np.int64) + tok0
    c, s = _rope_tables(pos, 32)
    tbc = np.concatenate([c, s], axis=1)
    row = pos // 64
    col = pos % 64
    cr, sr = _rope_tables(row, 16)
    cc, sc = _rope_tables(col, 16)
    ta = np.concatenate([cr, cc, sr, sc], axis=1)
    return _tile_layout(tbc), _tile_layout(ta)


def _mask_b():
    k = np.arange(128)[:, None]
    q = np.arange(128)[None, :]
    tiles = []
    for dlt in (-1, 0, 1):
        d = 128 * dlt + k - q
        tiles.append((np.abs(d) <= 128).astype(np.float32))
    return np.concatenate(tiles, axis=1)


def _mask_c():
    k = np.arange(128)[:, None]
    q = np.arange(128)[None, :]
    tiles = []
    for dlt in range(-8, 9):
        d = 128 * dlt + k - q
        m = (np.abs(d) <= 64).astype(np.float32)
        m += ((d % 4 == 0) & (np.abs(d) <= 256)).astype(np.float32)
        m += ((d % 16 == 0) & (np.abs(d) <= 1024)).astype(np.float32)
        tiles.append(m)
    return np.concatenate(tiles, axis=1)


D_CASES = [("int", list(range(-2, 3))), ("q0", list(range(-2, 4))), ("q1", list(range(-2, 3))),
           ("q14", list(range(-2, 3))), ("q15", list(range(-3, 3)))]
D_OFF = {}
_o = 0
for _n, _dl in D_CASES:
    D_OFF[_n] = _o
    _o += len(_dl)
D_NT = _o


def _d_valid(qtok, ktok, rows_total):
    qr, qc = qtok // 64, qtok % 64
    kr, kc = ktok // 64, ktok % 64
    rs = np.clip(qr - 4, 0, rows_total - 8)
    cs = np.clip(qc - 8, 0, 48)
    return ((kr >= rs) & (kr < rs + 8) & (kc >= cs) & (kc < cs + 16) & (ktok >= 0) & (ktok < rows_total * 64))


def _mask_d(tok0, total_tokens):
    rows_total = total_tokens // 64
    tiles = []
    k = np.arange(128)[:, None]
    q = np.arange(128)[None, :]
    for name, dl in D_CASES:
        qt = {"int": 4, "q0": 0, "q1": 1, "q14": 14, "q15": 15}[name]
        for dlt in dl:
            qtok = tok0 + qt * 128 + q
            ktok = tok0 + (qt + dlt) * 128 + k
            tiles.append(_d_valid(qtok, ktok, rows_total).astype(np.float32))
    return np.concatenate(tiles, axis=1)


def _rpb_toeplitz(d_rpb):
    L = d_rpb.shape[0]
    k = np.arange(128)[:, None]
    q = np.arange(128)[None, :]
    out = np.zeros((L, 128, 4, 7, 128), np.float32)
    for di, dlt in enumerate(range(-3, 4)):
        dr = (2 * dlt + k // 64 - q // 64) + 7
        dc = np.clip(k % 64 - q % 64, -15, 15) + 15
        ok = (dr >= 0) & (dr <= 14)
        drc = np.clip(dr, 0, 14)
        g = d_rpb[:, :, drc, dc]
        g = np.where(ok[None, None], g, np.float32(0.0))
        out[:, :, :, di, :] = g.transpose(0, 2, 1, 3)
    return np.ascontiguousarray(out.reshape(L, 128, 4 * 7 * 128))


def build_program():
    nc = bass.Bass("TRN2", target_bir_lowering=False)
    es = ExitStack()

    def din(name, shape, dt=F32):
        return nc.dram_tensor(name, list(shape), dt, kind="ExternalInput").ap()

    xin = din("xin", [5, T, D])
    mem = din("mem", [5, 256, D])
    norm_mix_g = din("norm_mix_g", [4, D])
    w_in = din("w_in", [4, D, 2560])
    aqg = din("aqg", [4, 64])
    akg = din("akg", [4, 64])
    bsink = din("bsink", [4, 4])
    grpg = din("grpg", [4, D])
    w_out = din("w_out", [4, D, D])
    norm_x_g = din("norm_x_g", [4, D])
    norm_mem_g = din("norm_mem_g", [4, D])
    w_xq = din("w_xq", [4, D, D])
    w_xkv = din("w_xkv", [4, D, 2 * D])
    w_xo = din("w_xo", [4, D, D])
    norm_ffn_g = din("norm_ffn_g", [4, D])
    w_up = din("w_up", [4, D, 2 * DFF])
    conv_w = din("conv_w", [4, 3, DFF])
    conv_b = din("conv_b", [4, DFF])
    w_down = din("w_down", [4, DFF, D])
    final_g = din("final_g", [1, D])
    rpbT = din("rpbT", [4, 128, 28 * 128])
    ident_d = din("ident", [128, 128])
    maskB_d = din("maskB", [128, 3 * 128])
    maskC_d = din("maskC", [128, 17 * 128])
    maskD_d = din("maskD", [2, 128, D_NT * 128])
    tabBC_d = din("tabBC", [2, 128, NT * 64])
    tabA_d = din("tabA", [2, 128, NT * 64])
    cvalid_d = din("cvalid", [128, 2])
    cinfo_d = din("cinfo", [1, 2], I32)
    yout = nc.dram_tensor("yout", [5, T, D], F32, kind="ExternalOutput").ap()
    expb_d = nc.dram_tensor("expb_scratch", [4, 128, 28 * 128], BF16).ap()

    GW = {"A": (1, 130), "B": (1, 130), "C": (2, 260), "D": (2, 260)}
    pack_d, gath_d = {}, {}
    for l in range(4):
        for g, (npr, vw) in GW.items():
            cols = npr * T + NT * vw
            pack_d[(l, g)] = nc.dram_tensor(f"pack_{l}{g}", [128, cols], BF16).ap()
            gath_d[(l, g)] = nc.dram_tensor(f"gath_{l}{g}", [NCORES * 128, cols], BF16).ap()
        pack_d[(l, "F")] = nc.dram_tensor(f"pack_{l}F", [128, 16], BF16).ap()
        gath_d[(l, "F")] = nc.dram_tensor(f"gath_{l}F", [NCORES * 128, 16], BF16).ap()

    def sb(name, shape, dt):
        return es.enter_context(nc.sbuf_tensor(name, list(shape), dt))

    def ps(name, shape, dt):
        return es.enter_context(nc.psum_tensor(name, list(shape), dt))

    def sem(name):
        return es.enter_context(nc.semaphore(name))

    S = SBoard()
    for en in ("pe", "act", "dve", "pool", "sp"):
        S.add_engine(en, sem("s_" + en))

    def dsem(name):
        d = DSem(sem("d_" + name))
        S.dsems.append(d)
        return d

    X = sb("X", [128, NT, D], F32)
    XNT = sb("XNT", [128, KC, T + 2], BF16)
    GBC = sb("GBC", [128, D], F32)
    GGRP = sb("GGRP", [128, 256], F32)
    XNB = sb("XNB", [128, 2, D], BF16)
    WS = sb("WS", [128, 2, 2048], BF16)
    IDB = sb("IDB", [128, 128], BF16)
    ONES = sb("ONES", [128, 128], BF16)
    SQJ = sb("SQJ", [128, D], BF16)
    SS = sb("SS", [128, 16], F32)
    LNV = sb("LNV", [128, 16], F32)
    RSTD = sb("RSTD", [128, 16], F32)
    SM = sb("SM", [128, 64], F32)
    CVAL = sb("CVAL", [128, 2], F32)
    CINF = sb("CINF", [1, 2], I32)
    QKG = sb("QKG", [128, 2, 64], F32)
    SINK = sb("SINK", [128, 4], F32)
    ESINK = sb("ESINK", [128, 4], F32)
    CW = sb("CW", [128, 4, NFC, 4], F32)
    AB_N = 37888
    AF_N = 3904
    AB = sb("AB", [128, AB_N], BF16)
    AFa = sb("AFa", [128, AF_N], F32)

    class Carve:
        def __init__(self, t, n):
            self.t, self.n, self.off = t, n, 0

        def reset(self):
            self.off = 0

        def take(self, n):
            a = self.t[:, self.off:self.off + n]
            self.off += n
            assert self.off <= self.n, (self.off, self.n)
            return a

    cb, cf = Carve(AB, AB_N), Carve(AFa, AF_N)
    QT = cb.take(2 * T).rearrange("p (a t) -> p a t", a=2)
    KT = cb.take(2 * 4096).rearrange("p (a t) -> p a t", a=2)
    VEf = cb.take(32 * 260)
    VE = VEf.rearrange("p (j w) -> p j w", j=32)
    WO = cb.take(2048).rearrange("p (k n) -> p k n", k=2)
    QKB = cb.take(1024).rearrange("p (a n) -> p a n", a=4)
    PT = cb.take(6 * 512).rearrange("p (a n) -> p a n", a=6)
    YB = cb.take(1024).rearrange("p (a n) -> p a n", a=4)
    YT = cb.take(512).rearrange("p (a k t) -> p a k t", a=2, k=2)
    MB = cb.take(3 * 128)
    MC = cb.take(17 * 128)
    MD01 = cb.take(D_NT * 128)
    EXPB = cb.take(28 * 128).rearrange("p (h d q) -> p h d q", h=4, d=7)
    TM = cf.take(512).rearrange("p (a n) -> p a n", a=2)
    RTA = cf.take(512).rearrange("p (a n) -> p a n", a=2)
    RTB = cf.take(512).rearrange("p (a n) -> p a n", a=2)
    YF = cf.take(256)
    TABBC = cf.take(1024).rearrange("p (t n) -> p t n", t=NT)
    TABA = cf.take(1024).rearrange("p (t n) -> p t n", t=NT)
    RPBS = None
    mix_b_end, mix_f_end = cb.off, cf.off
    cb.reset(); cf.reset()
    QX = cb.take(KC * T).rearrange("p (k t) -> p k t", k=KC)
    MK = cb.take(KC * 256).rearrange("p (k m) -> p k m", k=KC)
    MV = cb.take(2 * D).rearrange("p (j n) -> p j n", j=2)
    MNT = cb.take(KC * 256).rearrange("p (k m) -> p k m", k=KC)
    PTX = cb.take(4 * 512).rearrange("p (a n) -> p a n", a=4)
    MEMX = cf.take(2 * D).rearrange("p (j n) -> p j n", j=2)
    DENR = cf.take(1024).rearrange("p (a n) -> p a n", a=2)
    cb.reset(); cf.reset()
    HT = cb.take(NFC * 1024).rearrange("p (f t) -> p f t", f=NFC)
    WD = cb.take(2 * NFC * 256).rearrange("p (a f n) -> p a f n", a=2, f=NFC)
    UB = cb.take(1024)
    GST = cf.take(2 * 1026).rearrange("p (a n) -> p a n", a=2)
    CT = cf.take(1024)
    cf.reset()
    OUTS = cf.take(2 * D).rearrange("p (a n) -> p a n", a=2)
    FB = [ps(f"F{i}", [128, 512], F32) for i in range(7)]
    T0 = ps("T0", [128, 1024], BF16)

    def B(n):
        return Buf(n)

    Xb = [B(f"X{t}") for t in range(NT)]
    XNTb = [B(f"XNT{t}") for t in range(NT)]
    XNTh = B("XNTh")
    GBCb, GGRPb = B("GBC"), B("GGRP")
    XNBb = [B("XNB0"), B("XNB1")]
    WSb = [B("WS0"), B("WS1")]
    CONSTb = B("const")
    STATb = B("stat")
    QKGb, SINKb, CWb = B("QKG"), B("SINK"), B("CW")
    Fb = [Buf(f"F{i}", excl=True) for i in range(7)]
    _t0 = Buf("T0", excl=True)
    T0b = [_t0, _t0]
    QTb = [B(f"QT{t}") for t in range(NT)]
    KTb = [B(f"KT{j}") for j in range(32)]
    VEb = [B(f"VE{j}") for j in range(32)]
    WOb = B("WO")
    QKBb = [B(f"QKB{i}") for i in range(4)]
    PTb = [B(f"PT{i}") for i in range(6)]
    YBb, YFb = [B(f"YB{i}") for i in range(4)], B("YF")
    YTb = [B("YT0"), B("YT1")]
    MDb, EXPBb, TABb = B("MD01"), B("EXPB"), B("TAB")
    TMb = [B("TM0"), B("TM1")]
    RTb = [B("RT0"), B("RT1")]
    QXb = [B(f"QX{t}") for t in range(4)]
    MKb, MVb, MNTb, MEMXb = B("MK"), B("MV"), B("MNT"), B("MEMX")
    DENRb = [B("DENR0"), B("DENR1")]
    HTb = [B(f"HT{f}") for f in range(NFC)]
    WDb = [B("WD0"), B("WD1")]
    UBb, CTb = B("UB"), B("CT")
    GSTb = [B("GST0"), B("GST1")]
    OUTSb = [B("OUTS0"), B("OUTS1")]
    PACKb = {k: B(f"pack{k}") for k in pack_d}
    GATHb = {k: B(f"gath{k}") for k in gath_d}

    d_x = [dsem(f"x{t}") for t in range(NT)]
    d_ws = [dsem("ws0"), dsem("ws1")]
    d_wd = [dsem("wd0"), dsem("wd1")]
    d_misc = dsem("misc")
    d_gbc, d_ggrp, d_wo = dsem("gbc"), dsem("ggrp"), dsem("wo")
    d_out = [dsem("out0"), dsem("out1")]
    d_kv = [dsem("kv0"), dsem("kv1")]
    d_pack = dsem("pack")
    d_cc = dsem("cc")
    d_mem = dsem("mem")
    d_tab = dsem("tab")
    d_expb, d_qkg, d_sink = dsem("expb"), dsem("qkg"), dsem("sink")

    OP = S.op
    regs = {}
    STOP = os.environ.get("MK_STOP", "")

    def stop(name):
        if STOP == name:
            S.dry = True

    def mm(out, lhsT, rhs, start, stop, reads, writes):
        OP("pe", lambda e: e.matmul(out, lhsT=lhsT, rhs=rhs, start=start, stop=stop, skip_group_check=True),
           reads, writes)

    def tr(out, in_, reads, writes):
        OP("pe", lambda e: e.transpose(out, in_, IDB[:]), reads + [CONSTb], writes)

    def act(out, in_, func, reads, writes, **kw):
        OP("act", lambda e: e.activation(out=out, in_=in_, func=func, **kw), reads, writes)

    def dma(eng, out, in_, reads, writes, ds, **kw):
        OP(eng, lambda e: e.dma_start(out=out, in_=in_, **kw), reads, writes, dsem=ds)

    def tt(eng, out, in0, in1, op, reads, writes):
        OP(eng, lambda e: e.tensor_tensor(out=out, in0=in0, in1=in1, op=op), reads, writes)

    def ts(eng, out, in0, s1, s2, op0, op1, reads, writes, **kw):
        if op1 is None:
            OP(eng, lambda e: e.tensor_scalar(out=out, in0=in0, scalar1=s1, scalar2=None, op0=op0, **kw), reads, writes)
        else:
            OP(eng, lambda e: e.tensor_scalar(out=out, in0=in0, scalar1=s1, scalar2=s2, op0=op0, op1=op1, **kw),
               reads, writes)

    def stt(out, in0, scalar, in1, op0, op1, reads, writes, **kw):
        OP("dve", lambda e: e.scalar_tensor_tensor(out=out, in0=in0, scalar=scalar, in1=in1, op0=op0, op1=op1, **kw),
           reads, writes)

    wspecs = []
    wstate = {"i": 0, "issued": 0}

    def w_issue(i):
        if i >= len(wspecs):
            return
        slot = i % 2
        for (dst_fn, src) in wspecs[i]:
            dma("pool", dst_fn(slot), src, [], [WSb[slot]], d_ws[slot])

    def wreq(parts):
        i = wstate["i"]
        wstate["i"] += 1
        if S.dry:
            wspecs.append(parts)
            return i % 2
        while wstate["issued"] <= min(i + 1, len(wspecs) - 1):
            w_issue(wstate["issued"])
            wstate["issued"] += 1
        return i % 2

    def slab256(wap, l, c0, permq=False):
        if permq:
            parts = []
            for pr in range(2):
                for lh in range(2):
                    src = wap[l, :, c0:c0 + 256].rearrange("(k p) (lh pr d) -> p k pr lh d", p=128, lh=2, pr=2)[:, :, pr, lh, :]
                    parts.append((lambda slot, pr=pr, lh=lh: WS[:, slot, :].rearrange("p (k pr lh d) -> p k pr lh d", k=KC, pr=2, lh=2)[:, :, pr, lh, :], src))
            return parts
        src = wap[l, :, c0:c0 + 256].rearrange("(k p) n -> p k n", p=128)
        return [(lambda slot: WS[:, slot, :].rearrange("p (k n) -> p k n", k=KC), src)]

    def load_consts():
        dma("pool", IDB[:], ident_d[:, :], [], [CONSTb], d_misc)
        dma("sp", CVAL[:], cvalid_d[:, :], [], [CONSTb], d_misc)
        for l in range(4):
            for j in range(3):
                dma("sp", CW[:, l, :, j], conv_w[l, j, :].rearrange("(f p) -> p f", p=128), [], [CONSTb], d_misc,
                    allow_slow_non_contiguous=True)
            dma("sp", CW[:, l, :, 3], conv_b[l, :].rearrange("(f p) -> p f", p=128), [], [CONSTb], d_misc,
                allow_slow_non_contiguous=True)
        tmpb = Buf("rpbstage")
        for l in range(4):
            dma("sp", AFa[:, 0:3584], rpbT[l, :, :], [], [tmpb], d_tab)
            act(EXPB.rearrange("p h d q -> p (h d q)"), AFa[:, 0:3584], AF.Exp, [tmpb], [EXPBb])
            dma("sp", expb_d[l, :, :], EXPB.rearrange("p h d q -> p (h d q)"), [EXPBb], [], d_pack)
        OP("pool", lambda e: e.memset(ONES[:], 1.0), [], [CONSTb])
        OP("pool", lambda e: e.memset(XNT[:, :, 0:1], 0.0), [], [XNTh])
        OP("pool", lambda e: e.memset(XNT[:, :, T + 1:T + 2], 0.0), [], [XNTh])

    def load_seq_tables(smp):
        i = 1 if smp else 0
        dma("pool", MB, maskB_d[:, :], [], [MDb], d_tab)
        dma("pool", MC, maskC_d[:, :], [], [MDb], d_tab)
        dma("pool", MD01, maskD_d[i, :, :], [], [MDb], d_tab)
        dma("sp", TABBC.rearrange("p t n -> p (t n)"), tabBC_d[i, :, :], [], [TABb], d_tab)
        dma("sp", TABA.rearrange("p t n -> p (t n)"), tabA_d[i, :, :], [], [TABb], d_tab)

    def load_x(s):
        for t in range(NT):
            dma("sp", X[:, t, :], xin[s, t * 128:(t + 1) * 128, :], [], [Xb[t]], d_x[t])

    def norm_stats(src_fn, n, src_bufs):
        for t in range(n):
            act(SQJ[:], src_fn(t), AF.Square, [src_bufs[t]], [STATb], accum_out=SS[:, t:t + 1])
        act(LNV[:, 0:n], SS[:, 0:n], AF.Ln, [STATb], [STATb], scale=1.0 / D, bias=EPS)
        act(RSTD[:, 0:n], LNV[:, 0:n], AF.Exp, [STATb], [STATb], scale=-0.5)

    def norm_to_T(src_fn, n, src_bufs, g_ap, dst_fn, dst_bufs):
        dma("sp", GBC[:], g_ap.partition_broadcast(128), [], [GBCb], d_gbc)
        norm_stats(src_fn, n, src_bufs)
        for t in range(n):
            sl = t % 2
            stt(XNB[:, sl, :], src_fn(t), RSTD[:, t:t + 1], GBC[:], ALU.mult, ALU.mult,
                [src_bufs[t], STATb, GBCb], [XNBb[sl]])
            for hh in range(2):
                for k4 in range(4):
                    kc = hh * 4 + k4
                    tr(T0[:, kc * 128:(kc + 1) * 128], XNB[:, sl, kc * 128:(kc + 1) * 128], [XNBb[sl]], [T0b[hh]])
                src = T0[:, hh * 512:(hh + 1) * 512].rearrange("p (k t) -> p k t", k=4)
                dst = dst_fn(t)[:, hh * 4:(hh + 1) * 4, :]
                if hh == 0:
                    act(dst, src, AF.Copy, [T0b[hh]], [dst_bufs[t]])
                else:
                    OP("dve", lambda e, dst=dst, src=src: e.tensor_copy(out=dst, in_=src), [T0b[hh]], [dst_bufs[t]])

    fbrr = {"i": 0}

    def next_fb(lo=0, hi=4):
        i = lo + fbrr["i"] % (hi - lo)
        fbrr["i"] += 1
        return i

    def rope_bc(sl, qs, t, H):
        x3 = TM[:, sl, 0:H * 64].rearrange("p (h d) -> p h d", h=H)
        a3 = RTA[:, sl, 0:H * 64].rearrange("p (h d) -> p h d", h=H)
        b3 = RTB[:, sl, 0:H * 64].rearrange("p (h d) -> p h d", h=H)
        cos = TABBC[:, t, 0:32].unsqueeze(1).to_broadcast([128, H, 32])
        sin = TABBC[:, t, 32:64].unsqueeze(1).to_broadcast([128, H, 32])
        R, W = [TMb[sl], TABb], [RTb[sl]]
        tt("pool", a3[:, :, 0:32], x3[:, :, 0:32], cos, ALU.mult, R, W)
        tt("pool", a3[:, :, 32:64], x3[:, :, 32:64], cos, ALU.mult, R, W)
        tt("pool", b3[:, :, 0:32], x3[:, :, 32:64], sin, ALU.mult, R, W)
        tt("dve", b3[:, :, 32:64], x3[:, :, 0:32], sin, ALU.mult, R, W)
        o3 = QKB[:, qs, 0:H * 64].rearrange("p (h d) -> p h d", h=H)
        tt("dve", o3[:, :, 0:32], a3[:, :, 0:32], b3[:, :, 0:32], ALU.subtract, [RTb[sl]], [QKBb[qs]])
        tt("dve", o3[:, :, 32:64], a3[:, :, 32:64], b3[:, :, 32:64], ALU.add, [RTb[sl]], [QKBb[qs]])

    def rope_a(sl, qs, t, H, gi):
        x3 = TM[:, sl, 0:H * 64].rearrange("p (h d) -> p h d", h=H)
        a3 = RTA[:, sl, 0:H * 64].rearrange("p (h d) -> p h d", h=H)
        b3 = RTB[:, sl, 0:H * 64].rearrange("p (h d) -> p h d", h=H)
        R, W = [TMb[sl], TABb], [RTb[sl]]
        tt("pool", a3, x3, x3, ALU.mult, R, W)
        OP("dve", lambda e: e.tensor_reduce(out=SM[:, 0:H], in_=a3, axis=AX.X, op=ALU.add), [RTb[sl]], [STATb])
        act(SM[:, 8:8 + H], SM[:, 0:H], AF.Ln, [STATb], [STATb], scale=1.0 / 64, bias=EPS)
        act(SM[:, 16:16 + H], SM[:, 8:8 + H], AF.Exp, [STATb], [STATb], scale=-0.5)
        tt("dve", b3, x3, SM[:, 16:16 + H].unsqueeze(2).to_broadcast([128, H, 64]), ALU.mult, [TMb[sl], STATb], [RTb[sl]])
        tt("pool", x3, b3, QKG[:, gi, :].unsqueeze(1).to_broadcast([128, H, 64]), ALU.mult, [RTb[sl], QKGb], [TMb[sl]])
        x5 = TM[:, sl, 0:H * 64].rearrange("p (h r f d) -> p h r f d", h=H, r=2, f=2)
        a5 = RTA[:, sl, 0:H * 64].rearrange("p (h r f d) -> p h r f d", h=H, r=2, f=2)
        b5 = RTB[:, sl, 0:H * 64].rearrange("p (h r f d) -> p h r f d", h=H, r=2, f=2)
        cos = TABA[:, t, 0:32].rearrange("p (r d) -> p r d", r=2).unsqueeze(1).to_broadcast([128, H, 2, 16])
        sin = TABA[:, t, 32:64].rearrange("p (r d) -> p r d", r=2).unsqueeze(1).to_broadcast([128, H, 2, 16])
        tt("pool", a5[:, :, :, 0, :], x5[:, :, :, 0, :], cos, ALU.mult, R, W)
        tt("pool", a5[:, :, :, 1, :], x5[:, :, :, 1, :], cos, ALU.mult, R, W)
        tt("pool", b5[:, :, :, 0, :], x5[:, :, :, 1, :], sin, ALU.mult, R, W)
        tt("dve", b5[:, :, :, 1, :], x5[:, :, :, 0, :], sin, ALU.mult, R, W)
        o5 = QKB[:, qs, 0:H * 64].rearrange("p (h r f d) -> p h r f d", h=H, r=2, f=2)
        tt("dve", o5[:, :, :, 0, :], a5[:, :, :, 0, :], b5[:, :, :, 0, :], ALU.subtract, [RTb[sl]], [QKBb[qs]])
        tt("dve", o5[:, :, :, 1, :], a5[:, :, :, 1, :], b5[:, :, :, 1, :], ALU.add, [RTb[sl]], [QKBb[qs]])

    def proj_token_major(l, g, c0, kinds, smp):
        slot = wreq(slab256(w_in, l, c0, permq=(g in ("A", "B") and kinds[0][0] == "q")))
        wv = WS[:, slot, :].rearrange("p (k n) -> p k n", k=KC)
        pending = []
        for t in range(NT):
            fi = next_fb(0, 4)
            for kc in range(KC):
                mm(FB[fi][:, 0:256], XNT[:, kc, 1 + t * 128:1 + (t + 1) * 128], wv[:, kc, :], kc == 0, kc == KC - 1,
                   [XNTb[t], WSb[slot]], [Fb[fi]])
            off = 0
            for kind, ncol in kinds:
                if kind == "v":
                    nh = ncol // 64
                    dst = VE[:, 8 + t, 0:nh * 65].rearrange("p (h w) -> p h w", h=nh)[:, :, 0:64]
                    src = FB[fi][:, off:off + ncol].rearrange("p (h d) -> p h d", h=nh)
                    act(dst, src, AF.Copy, [Fb[fi]], [VEb[8 + t]])
                else:
                    H = ncol // 64
                    sl = t % 2
                    qs = t % 4
                    act(TM[:, sl, 0:ncol], FB[fi][:, off:off + ncol], AF.Copy, [Fb[fi]], [TMb[sl]])
                    if g == "A":
                        rope_a(sl, qs, t, H, 0 if kind == "q" else 1)
                    else:
                        rope_bc(sl, qs, t, H)

                    def fin(t=t, qs=qs, kind=kind, npair=ncol // 128):
                        for pr in range(npair):
                            tr(T0[:, pr * 128:(pr + 1) * 128], QKB[:, qs, pr * 128:(pr + 1) * 128], [QKBb[qs]], [T0b[0]])
                        src = T0[:, 0:npair * 128].rearrange("p (a t) -> p a t", a=npair)
                        if kind == "q":
                            OP("dve", lambda e: e.tensor_copy(out=QT[:, 0:npair, t * 128:(t + 1) * 128], in_=src), [T0b[0]], [QTb[t]])
                        else:
                            OP("dve", lambda e: e.tensor_copy(out=KT[:, 0:npair, (8 + t) * 128:(9 + t) * 128], in_=src),
                               [T0b[0]], [KTb[8 + t]])
                    pending.append(fin)
                off += ncol
            while len(pending) > 2:
                pending.pop(0)()
        for f in pending:
            f()

    def proj_feature_major(l, c0, which):
        slot = wreq(slab256(w_in, l, c0))
        wv = WS[:, slot, :].rearrange("p (k n) -> p k n", k=KC)
        for pr in range(2):
            for tb in range(4):
                fi = next_fb(0, 4)
                for kc in range(KC):
                    mm(FB[fi][:, 0:512], wv[:, kc, pr * 128:(pr + 1) * 128], XNT[:, kc, 1 + tb * 512:1 + (tb + 1) * 512],
                       kc == 0, kc == KC - 1, [XNTb[4 * tb + i] for i in range(4)] + [WSb[slot]], [Fb[fi]])
                if which == "q":
                    act(QT[:, pr, tb * 512:(tb + 1) * 512], FB[fi][:, 0:512], AF.Copy, [Fb[fi]], [QTb[4 * tb + i] for i in range(4)])
                else:
                    act(KT[:, pr, 1024 + tb * 512:1024 + (tb + 1) * 512], FB[fi][:, 0:512], AF.Copy, [Fb[fi]],
                        [KTb[8 + 4 * tb + i] for i in range(4)])

    def exchange(l, g, halo):
        npr, vw = GW[g]
        pk, gt = pack_d[(l, g)], gath_d[(l, g)]
        key = (l, g)
        own = list(range(8, 24))
        dma("sp", pk[:, 0:npr * T].rearrange("p (a t) -> p a t", a=npr), KT[:, 0:npr, 1024:3072],
            [KTb[j] for j in own], [PACKb[key]], d_pack)
        dma("sp", pk[:, npr * T:npr * T + NT * vw].rearrange("p (j w) -> p j w", j=NT), VE[:, 8:24, 0:vw],
            [VEb[j] for j in own], [PACKb[key]], d_pack)
        OP("pool", lambda e: e.collective_compute("AllGather", ALU.bypass, replica_groups=[list(range(NCORES))],
                                                   ins=[pk[:, :]], outs=[gt[:, :]]),
           [PACKb[key]], [GATHb[key]], dsem=d_cc, inc=-1)
        if halo == 0:
            return
        h = halo
        for side in range(2):
            def rows_fn(side=side):
                rowv = regs["left"] if side == 0 else regs["right"]
                return gt[bass.DynSlice(rowv, 128), :]
            if side == 0:
                ksrc = lambda rf=rows_fn: rf()[:, 0:npr * T].rearrange("p (a t) -> p a t", a=npr)[:, :, T - h * 128:T]
                kdst = KT[:, 0:npr, (8 - h) * 128:1024]
                vsrc = lambda rf=rows_fn: rf()[:, npr * T:npr * T + NT * vw].rearrange("p (j w) -> p j w", j=NT)[:, NT - h:NT, :]
                vdst = VE[:, 8 - h:8, 0:vw]
                js = list(range(8 - h, 8))
            else:
                ksrc = lambda rf=rows_fn: rf()[:, 0:npr * T].rearrange("p (a t) -> p a t", a=npr)[:, :, 0:h * 128]
                kdst = KT[:, 0:npr, 3072:3072 + h * 128]
                vsrc = lambda rf=rows_fn: rf()[:, npr * T:npr * T + NT * vw].rearrange("p (j w) -> p j w", j=NT)[:, 0:h, :]
                vdst = VE[:, 24:24 + h, 0:vw]
                js = list(range(24, 24 + h))
            OP("sp", lambda e, kdst=kdst, ksrc=ksrc: e.dma_start(out=kdst, in_=ksrc()), [GATHb[key]], [KTb[j] for j in js], dsem=d_kv[side])
            OP("sp", lambda e, vdst=vdst, vsrc=vsrc: e.dma_start(out=vdst, in_=vsrc()), [GATHb[key]], [VEb[j] for j in js], dsem=d_kv[side])
            ts("pool", vdst, vdst, CVAL[:, side:side + 1], None, ALU.mult, None, [VEb[j] for j in js] + [CONSTb],
               [VEb[j] for j in js])

    def attention(l, g, smp):
        gqa = g in ("A", "B")
        npr, vw = GW[g]
        QB = 2 if (g == "A" and smp) else 1
        nblk = 512 // (QB * 128)
        gidx = "ABCD".index(g)
        stream = smp and g == "A"

        def heads_of(pr):
            if gqa:
                return (pr, pr + 2), (0, 1)
            return (2 * pr, 2 * pr + 1), (2 * pr, 2 * pr + 1)

        def key_tiles(i):
            if g == "A":
                return list(range(NT)), None
            if g == "B":
                lo, hi, moff, d0 = i - 1, i + 1, 0, -1
            elif g == "C":
                lo, hi, moff, d0 = i - 8, i + 8, 0, -8
            else:
                case = {0: "q0", 1: "q1", 14: "q14", 15: "q15"}.get(i, "int")
                dl = dict(D_CASES)[case]
                lo, hi, moff, d0 = i + dl[0], i + dl[-1], D_OFF[case], dl[0]
            js = [j for j in range(lo, hi + 1) if smp or 0 <= j < NT]
            return js, (moff, d0)

        deferred = []
        BANKP = [(0, 1), (2, 3)] if QB == 2 else [(0, 1), (2, 3), (5, 6)]
        NBP = len(BANKP)
        rot = {"i": 0}
        for qb in range(NT // QB):
            qts = [qb * QB + a for a in range(QB)]
            accs = [4 + a for a in range(QB)] if QB == 2 else [4]
            first_acc = {a: True for a in accs}
            if stream:
                chunks = list(range(NCORES))
            else:
                chunks = [None]
            for ch in chunks:
                if stream:
                    sidx = ch % 2
                    gt = gath_d[(l, g)]
                    rows = gt[ch * 128:(ch + 1) * 128, :]
                    jj = [sidx * 16 + i for i in range(NT)]
                    dma("sp", KT[:, 0, sidx * T:(sidx + 1) * T], rows[:, 0:T], [GATHb[(l, g)]], [KTb[j] for j in jj], d_kv[sidx])
                    dma("sp", VE[:, sidx * 16:(sidx + 1) * 16, 0:vw], rows[:, T:T + NT * vw].rearrange("p (j w) -> p j w", j=NT),
                        [GATHb[(l, g)]], [VEb[j] for j in jj], d_kv[sidx])
                    kslots = [(sidx * 16 + i, None) for i in range(NT)]
                else:
                    js, minfo = key_tiles(qts[0])
                    kslots = [(8 + j, (minfo[0] + (j - qts[0]) - minfo[1]) if minfo else None) for j in js]
                groups = [kslots[a:a + nblk] for a in range(0, len(kslots), nblk)]
                work = [(pr, grp) for grp in groups for pr in range(2)]
                pend = []
                for wi, (pr, grp) in enumerate(work):
                    (hlo, hhi), (klo, khi) = heads_of(pr)
                    sb_i = rot["i"] % NBP
                    rot["i"] += 1
                    banks = BANKP[sb_i]
                    pts = (2 * sb_i, 2 * sb_i + 1)
                    kpair = 0 if gqa else pr
                    qcols = (qts[0] * 128, (qts[-1] + 1) * 128)
                    qn = QB * 128
                    for bi, (slot_j, mi) in enumerate(grp):
                        for half, base in ((0, 0), (1, 64)):
                            mm(FB[banks[half]][:, bi * qn:(bi + 1) * qn],
                               KT[base:base + 64, kpair, slot_j * 128:(slot_j + 1) * 128],
                               QT[base:base + 64, pr, qcols[0]:qcols[1]], True, True,
                               [KTb[slot_j]] + [QTb[q] for q in qts], [Fb[banks[half]]])
                    ncol = len(grp) * qn
                    for half in range(2):
                        act(PT[:, pts[half], 0:ncol], FB[banks[half]][:, 0:ncol], AF.Exp, [Fb[banks[half]]], [PTb[pts[half]]],
                            scale=0.125)
                        if g in ("B", "C"):
                            m0 = grp[0][1]
                            msk = (MB if g == "B" else MC)[:, m0 * 128:(m0 + len(grp)) * 128]
                            tt("pool" if half == 0 else "dve", PT[:, pts[half], 0:ncol], PT[:, pts[half], 0:ncol], msk, ALU.mult,
                               [PTb[pts[half]], MDb], [PTb[pts[half]]])
                        elif g == "D":
                            m0 = grp[0][1]
                            hq = (hlo, hhi)[half]
                            i0 = qts[0]
                            case = {0: "q0", 1: "q1", 14: "q14", 15: "q15"}.get(i0, "int")
                            d0 = dict(D_CASES)[case][0]
                            dfirst = d0 + (m0 - D_OFF[case])
                            eb = EXPB[:, hq, dfirst + 3:dfirst + 3 + len(grp), :].rearrange("p d q -> p (d q)")
                            tt("pool" if half == 0 else "dve", PT[:, pts[half], 0:ncol], PT[:, pts[half], 0:ncol], eb, ALU.mult,
                               [PTb[pts[half]], EXPBb], [PTb[pts[half]]])
                            tt("pool" if half == 0 else "dve", PT[:, pts[half], 0:ncol], PT[:, pts[half], 0:ncol], MD01[:, m0 * 128:(m0 + len(grp)) * 128],
                               ALU.mult, [PTb[pts[half]], MDb], [PTb[pts[half]]])
                    pend.append((grp, pts, (hlo, hhi), (klo, khi), accs, first_acc, QB, qn, vw))
                    while len(pend) > NBP - 1:
                        emit_pv(*pend.pop(0))
                for pp in pend:
                    emit_pv(*pp)
            wbank = (lambda: 6) if QB == 2 else (lambda: BANKP[(rot.__setitem__("i", rot["i"] + 1) or rot["i"] - 1) % NBP][0])
            fins = [epilogue(l, g, gidx, qt, accs[a], wbank) for a, qt in enumerate(qts)]
            for f in deferred:
                f()
            deferred = fins
        for f in deferred:
            f()

    def emit_pv(grp, pts, hq, hk, accs, first_acc, QB, qn, vw):
        for bi, (slot_j, mi) in enumerate(grp):
            for half in range(2):
                h, kvh = hq[half], hk[half]
                for a in range(QB):
                    acc = accs[a]
                    st = first_acc[acc]
                    first_acc[acc] = False
                    mm(FB[acc][:, h * 65:(h + 1) * 65],
                       PT[:, pts[half], bi * qn + a * 128:bi * qn + (a + 1) * 128],
                       VE[:, slot_j, kvh * 65:(kvh + 1) * 65], st, True,
                       [PTb[pts[half]], VEb[slot_j]], [Fb[acc]])

    def epilogue(l, g, gidx, qt, acc, wbank):
        a3 = FB[acc][:, 0:260].rearrange("p (h w) -> p h w", h=4)
        if g == "B":
            tt("dve", SM[:, 24:28], a3[:, :, 64], ESINK[:, 0:4], ALU.add, [Fb[acc], SINKb], [STATb])
            OP("dve", lambda e: e.reciprocal(out=SM[:, 28:32], in_=SM[:, 24:28]), [STATb], [STATb])
        else:
            OP("dve", lambda e: e.reciprocal(out=SM[:, 28:32], in_=a3[:, :, 64]), [Fb[acc]], [STATb])
        y3 = YF.rearrange("p (h d) -> p h d", h=4)
        tt("dve", y3, a3[:, :, 0:64], SM[:, 28:32].unsqueeze(2).to_broadcast([128, 4, 64]), ALU.mult, [Fb[acc], STATb], [YFb])
        stt(SQJ[:, 0:256], YF, 1.0, YF, ALU.mult, ALU.mult, [YFb], [STATb], accum_out=SM[:, 32:33])
        act(SM[:, 33:34], SM[:, 32:33], AF.Ln, [STATb], [STATb], scale=1.0 / 256, bias=EPS)
        act(SM[:, 34:35], SM[:, 33:34], AF.Exp, [STATb], [STATb], scale=-0.5)
        yb4 = qt % 4
        stt(YB[:, yb4, :], YF, SM[:, 34:35], GGRP[:], ALU.mult, ALU.mult, [YFb, STATb, GGRPb], [YBb[yb4]])

        def part2():
            ys = qt % 2
            for kc in range(2):
                tr(T0[:, 512 + kc * 128:512 + (kc + 1) * 128], YB[:, yb4, kc * 128:(kc + 1) * 128], [YBb[yb4]], [T0b[1]])
            act(YT[:, ys, :, :], T0[:, 512:768].rearrange("p (k t) -> p k t", k=2), AF.Copy, [T0b[1]], [YTb[ys]])
            for hf in range(2):
                wb = wbank()
                for kc in range(2):
                    mm(FB[wb][:, 0:512], YT[:, ys, kc, :], WO[:, kc, hf * 512:(hf + 1) * 512], kc == 0, kc == 1,
                       [YTb[ys], WOb], [Fb[wb]])
                tt("dve", X[:, qt, hf * 512:(hf + 1) * 512], X[:, qt, hf * 512:(hf + 1) * 512], FB[wb][:, 0:512], ALU.add,
                   [Fb[wb], Xb[qt]], [Xb[qt]])
        return part2

    def mixer(s, l, smp):
        stop("loadx")
        load_seq_tables(smp)
        norm_to_T(lambda t: X[:, t, :], NT, Xb, norm_mix_g[l:l + 1, :], lambda t: XNT[:, :, 1 + t * 128:1 + (t + 1) * 128], XNTb)
        stop("norm")
        dma("sp", QKG[:, 0, :], aqg[l:l + 1, :].partition_broadcast(128), [], [QKGb], d_qkg)
        dma("sp", QKG[:, 1, :], akg[l:l + 1, :].partition_broadcast(128), [], [QKGb], d_qkg)
        dma("sp", SINK[:], bsink[l:l + 1, :].partition_broadcast(128), [], [SINKb], d_sink)
        act(ESINK[:], SINK[:], AF.Exp, [SINKb], [SINKb])
        dma("sp", EXPB.rearrange("p h d q -> p (h d q)"), expb_d[l, :, :], [], [EXPBb], d_expb)
        for g in "ABCD":
            gi = "ABCD".index(g)
            nv = 2 if g in ("A", "B") else 4
            OP("pool", lambda e, nv=nv: e.memset(VE[:, 8:24, 0:nv * 65].rearrange("p j (h w) -> p j h w", h=nv)[:, :, :, 64:65], 1.0),
               [], [VEb[j] for j in range(8, 24)])
            c0 = gi * 512 if gi < 2 else 1024 + (gi - 2) * 768
            if g in ("A", "B"):
                proj_token_major(l, g, c0, [("q", 256)], smp)
                proj_token_major(l, g, c0 + 256, [("k", 128), ("v", 128)], smp)
            elif g == "C":
                proj_token_major(l, g, c0, [("q", 256)], smp)
                proj_token_major(l, g, c0 + 256, [("k", 256)], smp)
                proj_token_major(l, g, c0 + 512, [("v", 256)], smp)
            else:
                proj_feature_major(l, c0, "q")
                proj_feature_major(l, c0 + 256, "k")
                proj_token_major(l, g, c0 + 512, [("v", 256)], smp)
            dma("sp", GGRP[:], grpg[l:l + 1, gi * 256:(gi + 1) * 256].partition_broadcast(128), [], [GGRPb], d_ggrp)
            dma("pool", WO, w_out[l, gi * 256:(gi + 1) * 256, :].rearrange("(k p) n -> p k n", p=128), [], [WOb], d_wo)
            stop("proj" + g)
            if smp:
                exchange(l, g, {"A": 0, "B": 1, "C": 8, "D": 2}[g])
            stop("exch" + g)
            attention(l, g, smp)
            stop("attn" + g)

    def cross(s, l, smp):
        dma("sp", MEMX, mem[s, :, :].rearrange("(j p) n -> p j n", p=128), [], [MEMXb], d_mem)
        norm_to_T(lambda j: MEMX[:, j, :], 2, [MEMXb, MEMXb], norm_mem_g[l:l + 1, :],
                  lambda j: MNT[:, :, j * 128:(j + 1) * 128], [MNTb, MNTb])
        for sl4 in range(4):
            slot = wreq(slab256(w_xkv, l, sl4 * 256))
            wv = WS[:, slot, :].rearrange("p (k n) -> p k n", k=KC)
            for j in range(2):
                fi = next_fb(0, 4)
                for kc in range(KC):
                    mm(FB[fi][:, 0:256], wv[:, kc, j * 128:(j + 1) * 128], MNT[:, kc, :], kc == 0, kc == KC - 1,
                       [MNTb, WSb[slot]], [Fb[fi]])
                act(MK[:, 2 * sl4 + j, :], FB[fi][:, 0:256], AF.Copy, [Fb[fi]], [MKb])
        for sl4 in range(4):
            slot = wreq(slab256(w_xkv, l, D + sl4 * 256))
            wv = WS[:, slot, :].rearrange("p (k n) -> p k n", k=KC)
            for mt in range(2):
                fi = next_fb(0, 4)
                for kc in range(KC):
                    mm(FB[fi][:, 0:256], MNT[:, kc, mt * 128:(mt + 1) * 128], wv[:, kc, :], kc == 0, kc == KC - 1,
                       [MNTb, WSb[slot]], [Fb[fi]])
                act(MV[:, mt, sl4 * 256:(sl4 + 1) * 256], FB[fi][:, 0:256], AF.Copy, [Fb[fi]], [MVb])
        norm_to_T(lambda t: X[:, t, :], NT, Xb, norm_x_g[l:l + 1, :], lambda t: XNT[:, :, 1 + t * 128:1 + (t + 1) * 128], XNTb)
        for sl4 in range(4):
            slot = wreq(slab256(w_xq, l, sl4 * 256))
            wv = WS[:, slot, :].rearrange("p (k n) -> p k n", k=KC)
            for j in range(2):
                for tb in range(4):
                    fi = next_fb(0, 2)
                    for kc in range(KC):
                        mm(FB[fi][:, 0:512], wv[:, kc, j * 128:(j + 1) * 128], XNT[:, kc, 1 + tb * 512:1 + (tb + 1) * 512],
                           kc == 0, kc == KC - 1, [XNTb[4 * tb + i] for i in range(4)] + [WSb[slot]], [Fb[fi]])
                    act(QX[:, 2 * sl4 + j, tb * 512:(tb + 1) * 512], FB[fi][:, 0:512], AF.Copy, [Fb[fi]], [QXb[tb]])
        for tb in range(4):
            for h in range(4):
                for mt in range(2):
                    for dc in range(2):
                        mm(FB[2 + mt][:, 0:512], MK[:, 2 * h + dc, mt * 128:(mt + 1) * 128], QX[:, 2 * h + dc, tb * 512:(tb + 1) * 512],
                           dc == 0, dc == 1, [MKb, QXb[tb]], [Fb[2 + mt]])
                    pi = (h % 2) * 2 + mt
                    act(PTX[:, pi, :], FB[2 + mt][:, 0:512], AF.Exp, [Fb[2 + mt]], [PTb[pi]], scale=1.0 / 16)
                for mt in range(2):
                    pi = (h % 2) * 2 + mt
                    mm(FB[4][:, 0:512], ONES[:], PTX[:, pi, :], mt == 0, mt == 1, [PTb[pi], CONSTb], [Fb[4]])
                for dc in range(2):
                    for mt in range(2):
                        pi = (h % 2) * 2 + mt
                        mm(FB[5 + dc][:, 0:512], MV[:, mt, (2 * h + dc) * 128:(2 * h + dc + 1) * 128], PTX[:, pi, :],
                           mt == 0, mt == 1, [PTb[pi], MVb], [Fb[5 + dc]])
                dr = h % 2
                OP("dve", lambda e, dr=dr: e.reciprocal(out=DENR[:, dr, :], in_=FB[4][:, 0:512]), [Fb[4]], [DENRb[dr]])
                for dc in range(2):
                    tt("dve", XNT[:, 2 * h + dc, 1 + tb * 512:1 + (tb + 1) * 512], FB[5 + dc][:, 0:512], DENR[:, dr, :], ALU.mult,
                       [Fb[5 + dc], DENRb[dr]], [XNTb[4 * tb + i] for i in range(4)])
        for sl4 in range(4):
            slot = wreq(slab256(w_xo, l, sl4 * 256))
            wv = WS[:, slot, :].rearrange("p (k n) -> p k n", k=KC)
            for t in range(NT):
                fi = next_fb(0, 4)
                for kc in range(KC):
                    mm(FB[fi][:, 0:256], XNT[:, kc, 1 + t * 128:1 + (t + 1) * 128], wv[:, kc, :], kc == 0, kc == KC - 1,
                       [XNTb[t], WSb[slot]], [Fb[fi]])
                tt("dve", X[:, t, sl4 * 256:(sl4 + 1) * 256], X[:, t, sl4 * 256:(sl4 + 1) * 256], FB[fi][:, 0:256], ALU.add,
                   [Fb[fi], Xb[t]], [Xb[t]])

    def ffn(s, l, smp):
        norm_to_T(lambda t: X[:, t, :], NT, Xb, norm_ffn_g[l:l + 1, :], lambda t: XNT[:, :, 1 + t * 128:1 + (t + 1) * 128], XNTb)
        if smp:
            key = (l, "F")
            pk, gt = pack_d[key], gath_d[key]
            dma("sp", pk[:, 0:8], XNT[:, :, 1], [XNTb[0]], [PACKb[key]], d_pack, allow_slow_non_contiguous=True)
            dma("sp", pk[:, 8:16], XNT[:, :, T], [XNTb[NT - 1]], [PACKb[key]], d_pack, allow_slow_non_contiguous=True)
            OP("pool", lambda e: e.collective_compute("AllGather", ALU.bypass, replica_groups=[list(range(NCORES))],
                                                       ins=[pk[:, :]], outs=[gt[:, :]]),
               [PACKb[key]], [GATHb[key]], dsem=d_cc, inc=-1)
            OP("sp", lambda e: e.dma_start(out=XNT[:, :, 0], in_=gt[bass.DynSlice(regs["left"], 128), 8:16],
                                           allow_slow_non_contiguous=True), [GATHb[key]], [XNTh], dsem=d_kv[0])
            OP("sp", lambda e: e.dma_start(out=XNT[:, :, T + 1], in_=gt[bass.DynSlice(regs["right"], 128), 0:8],
                                           allow_slow_non_contiguous=True), [GATHb[key]], [XNTh], dsem=d_kv[1])
            ts("pool", XNT[:, :, 0], XNT[:, :, 0], CVAL[:, 0:1], None, ALU.mult, None, [XNTh, CONSTb], [XNTh])
            ts("pool", XNT[:, :, T + 1], XNT[:, :, T + 1], CVAL[:, 1:2], None, ALU.mult, None, [XNTh, CONSTb], [XNTh])
        else:
            OP("pool", lambda e: e.memset(XNT[:, :, 0:1], 0.0), [], [XNTh])
            OP("pool", lambda e: e.memset(XNT[:, :, T + 1:T + 2], 0.0), [], [XNTh])
        wd_i = {"i": 0}

        def wd_load(nb):
            slot = wd_i["i"] % 2
            wd_i["i"] += 1
            dma("pool", WD[:, slot, :, :], w_down[l, :, nb * 256:(nb + 1) * 256].rearrange("(f p) n -> p f n", p=128),
                [], [WDb[slot]], d_wd[slot])
            return slot

        for hf in range(2):
            tiles = list(range(hf * 8, hf * 8 + 8))
            gtiles = list(range(max(0, hf * 8 - 1), min(NT, hf * 8 + 9)))
            wd_load(0)
            wd_load(1)
            for f in range(NFC):
                parts = [(lambda slot: WS[:, slot, 0:1024].rearrange("p (k n) -> p k n", k=KC),
                          w_up[l, :, f * 128:(f + 1) * 128].rearrange("(k p) n -> p k n", p=128)),
                         (lambda slot: WS[:, slot, 1024:2048].rearrange("p (k n) -> p k n", k=KC),
                          w_up[l, :, DFF + f * 128:DFF + (f + 1) * 128].rearrange("(k p) n -> p k n", p=128))]
                slot = wreq(parts)
                wg = WS[:, slot, 0:1024].rearrange("p (k n) -> p k n", k=KC)
                wvv = WS[:, slot, 1024:2048].rearrange("p (k n) -> p k n", k=KC)
                gs = f % 2
                for gi3 in range(3):
                    c0 = hf * 1024 + gi3 * 342
                    for kc in range(KC):
                        mm(FB[gi3][:, 0:342], wg[:, kc, :], XNT[:, kc, c0:c0 + 342], kc == 0, kc == KC - 1,
                           [XNTb[t] for t in gtiles] + [XNTh, WSb[slot]], [Fb[gi3]])
                    act(GST[:, gs, gi3 * 342:(gi3 + 1) * 342], FB[gi3][:, 0:342], AF.Copy, [Fb[gi3]], [GSTb[gs]])
                vb = 3 + 2 * (f % 2)
                for vi in range(2):
                    c0 = hf * 1024 + 1 + vi * 512
                    for kc in range(KC):
                        mm(FB[vb + vi][:, 0:512], wvv[:, kc, :], XNT[:, kc, c0:c0 + 512], kc == 0, kc == KC - 1,
                           [XNTb[t] for t in tiles] + [WSb[slot]], [Fb[vb + vi]])
                ts("dve", CT, GST[:, gs, 1:1025], CW[:, l, f, 1:2], CW[:, l, f, 3:4], ALU.mult, ALU.add, [GSTb[gs], CONSTb], [CTb])
                stt(CT, GST[:, gs, 0:1024], CW[:, l, f, 0:1], CT, ALU.mult, ALU.add, [GSTb[gs], CONSTb, CTb], [CTb])
                stt(CT, GST[:, gs, 2:1026], CW[:, l, f, 2:3], CT, ALU.mult, ALU.add, [GSTb[gs], CONSTb, CTb], [CTb])
                act(UB, CT, AF.Gelu, [CTb], [UBb])
                for vi in range(2):
                    tt("dve", HT[:, f, vi * 512:(vi + 1) * 512], UB[:, vi * 512:(vi + 1) * 512], FB[vb + vi][:, 0:512], ALU.mult,
                       [UBb, Fb[vb + vi]], [HTb[f]])
            for nb in range(4):
                slot = nb % 2
                for tl in range(8):
                    fi = next_fb(0, 3)
                    for f in range(NFC):
                        mm(FB[fi][:, 0:256], HT[:, f, tl * 128:(tl + 1) * 128], WD[:, slot, f, :], f == 0, f == NFC - 1,
                           [HTb[f], WDb[slot]], [Fb[fi]])
                    t = hf * 8 + tl
                    tt("dve", X[:, t, nb * 256:(nb + 1) * 256], X[:, t, nb * 256:(nb + 1) * 256], FB[fi][:, 0:256], ALU.add,
                       [Fb[fi], Xb[t]], [Xb[t]])
                if nb + 2 < 4:
                    wd_load(nb + 2)

    def final_out(s):
        dma("sp", GBC[:], final_g[0:1, :].partition_broadcast(128), [], [GBCb], d_gbc)
        norm_stats(lambda t: X[:, t, :], NT, Xb)
        for t in range(NT):
            sl = t % 2
            stt(OUTS[:, sl, :], X[:, t, :], RSTD[:, t:t + 1], GBC[:], ALU.mult, ALU.mult, [Xb[t], STATb, GBCb], [OUTSb[sl]])
            dma("sp", yout[s, t * 128:(t + 1) * 128, :], OUTS[:, sl, :], [OUTSb[sl]], [], d_out[sl])

    def gen():
        wstate["i"] = 0
        fbrr["i"] = 0
        load_consts()
        stop("consts")
        prev_smp = None
        for s in range(SEQ0, SEQ0 + NSEQ):
            smp = (s == 4)
            if smp != prev_smp:
                S.barrier()
                prev_smp = smp
            load_x(s)
            for l in range(NLAYER):
                mixer(s, l, smp)
                S.barrier()
                stop("mixer")
                cross(s, l, smp)
                stop("cross")
                S.barrier()
                ffn(s, l, smp)
                stop("ffn")
                S.barrier()
            final_out(s)
            S.barrier()

    S.dry = True
    gen()
    S.dry = False
    gen()
    S.final_wait("sp")
    block = es.enter_context(nc.Block())
    regsem = sem("regsem")

    @block.sync
    def _(e):
        e.dma_start(out=CINF[:], in_=cinfo_d[:, :]).then_inc(regsem, 16)
        e.wait_ge(regsem, 16)
        rl = e.register("rleft").__enter__()
        rr = e.register("rright").__enter__()
        e.reg_load(rl, CINF[0:1, 0:1])
        e.reg_load(rr, CINF[0:1, 1:2])
        regs["left"] = e.snap(rl, min_val=0, max_val=NCORES - 1) * 128
        regs["right"] = e.snap(rr, min_val=0, max_val=NCORES - 1) * 128
        replay(e, S.engs["sp"].prog)

    @block.tensor
    def _(e):
        replay(e, S.engs["pe"].prog)

    @block.scalar
    def _(e):
        replay(e, S.engs["act"].prog)

    @block.vector
    def _(e):
        replay(e, S.engs["dve"].prog)

    @block.gpsimd
    def _(e):
        replay(e, S.engs["pool"].prog)

    es.close()
    ninstr = {k: len(v.prog) for k, v in S.engs.items()}
    return nc, ninstr


_CACHE = {}


def _f32(a):
    return np.ascontiguousarray(np.asarray(a, dtype=np.float32))


def kernel(x_prompt, x_sample, mem_prompt, mem_sample, norm_mix_g, w_in, a_q_norm_g, a_k_norm_g, b_sink, d_rpb,
           grp_norm_g, w_out, norm_x_g, norm_mem_g, w_xq, w_xkv, w_xo, norm_ffn_g, w_up, conv_w, conv_b, w_down,
           final_norm_g):
    if "nc" not in _CACHE:
        _CACHE["nc"] = build_program()
    nc, ninstr = _CACHE["nc"]
    x_prompt, x_sample, mem_prompt, mem_sample = map(_f32, (x_prompt, x_sample, mem_prompt, mem_sample))
    shared = dict(
        norm_mix_g=_f32(norm_mix_g), w_in=_f32(w_in), aqg=_f32(a_q_norm_g), akg=_f32(a_k_norm_g), bsink=_f32(b_sink),
        grpg=_f32(grp_norm_g).reshape(4, D), w_out=_f32(w_out), norm_x_g=_f32(norm_x_g), norm_mem_g=_f32(norm_mem_g),
        w_xq=_f32(w_xq), w_xkv=_f32(w_xkv), w_xo=_f32(w_xo), norm_ffn_g=_f32(norm_ffn_g), w_up=_f32(w_up),
        conv_w=_f32(conv_w), conv_b=_f32(conv_b), w_down=_f32(w_down), final_g=_f32(final_norm_g).reshape(1, D),
        rpbT=_rpb_toeplitz(_f32(d_rpb)), ident=np.eye(128, dtype=np.float32), maskB=_mask_b(), maskC=_mask_c(),
    )
    tb0, ta0 = _tables(0)
    md0 = _mask_d(0, T)
    in_maps = []
    for c in range(NCORES):
        tb1, ta1 = _tables(T * c)
        m = dict(shared)
        m["xin"] = np.ascontiguousarray(np.concatenate([x_prompt[4 * c:4 * c + 4], x_sample[0, T * c:T * (c + 1)][None]], axis=0))
        m["mem"] = np.ascontiguousarray(np.concatenate([mem_prompt[4 * c:4 * c + 4], mem_sample[0:1]], axis=0))
        m["maskD"] = np.ascontiguousarray(np.stack([md0, _mask_d(T * c, T * NCORES)]))
        m["tabBC"] = np.ascontiguousarray(np.stack([tb0, tb1]))
        m["tabA"] = np.ascontiguousarray(np.stack([ta0, ta1]))
        m["cvalid"] = np.ascontiguousarray(np.tile(np.array([[1.0 if c > 0 else 0.0, 1.0 if c < NCORES - 1 else 0.0]], np.float32), (128, 1)))
        m["cinfo"] = np.array([[max(c - 1, 0), min(c + 1, NCORES - 1)]], np.int32)
        in_maps.append(m)
    res = run_bass_kernel_spmd(nc, in_maps, core_ids=list(range(NCORES)))
    _CACHE["last_ns"] = getattr(res, "exec_time_ns", None)
    y_prompt = np.empty((32, T, D), np.float32)
    y_sample = np.empty((1, T * NCORES, D), np.float32)
    for c in range(NCORES):
        yo = res.results[c]["yout"]
        y_prompt[4 * c:4 * c + 4] = yo[0:4]
        y_sample[0, T * c:T * (c + 1)] = yo[4]
    return y_prompt, y_sample
```
